# Optimizing a Trainium2 kernel written in Bass

```python
import jax, jax.numpy as jnp
from jax import lax
import numpy as np

D_MODEL = 1024
BATCH = 2
SEQ = 16384
DEPTH = 2
DEC_BATCH = 8
DEC_SEQ = 64
PAST_LEN = 2048

CHUNK = 64
LEFT_CHUNKS = 8
LEFT_CTX = LEFT_CHUNKS * CHUNK
BAND = LEFT_CTX + CHUNK
N_HEADS = 16
HEAD_DIM = D_MODEL // N_HEADS
REL_CLIP = 128
N_REL = 2 * REL_CLIP + 1
CONV_WIDTH = 31
D_FF = ((8 * D_MODEL // 3 + 127) // 128) * 128
N_ATTN_LAYERS = (DEPTH + 1) // 2
N_CONV_LAYERS = DEPTH // 2
EPS = 1e-6
NEG_INF = -1e30

kernel_name = 'chunk_stream_conformer_hybrid'


def rms_norm(x, g):
    xf = x.astype(jnp.float32)
    y = xf * lax.rsqrt(jnp.mean(xf * xf, axis=-1, keepdims=True) + EPS)
    return (y * g.astype(jnp.float32)).astype(x.dtype)


def layer_norm(x, g, b):
    xf = x.astype(jnp.float32)
    mu = jnp.mean(xf, axis=-1, keepdims=True)
    xc = xf - mu
    y = xc * lax.rsqrt(jnp.mean(xc * xc, axis=-1, keepdims=True) + EPS)
    return (y * g.astype(jnp.float32) + b.astype(jnp.float32)).astype(x.dtype)


def swiglu_ffn(x, g, w_gate, w_up, w_down):
    h = rms_norm(x, g)
    return (jax.nn.silu(h @ w_gate) * (h @ w_up)) @ w_down


def rel_bias(table):
    i = jnp.arange(CHUNK)[:, None]
    j = jnp.arange(BAND)[None, :]
    dist = LEFT_CTX + i - j
    idx = jnp.clip(dist, -REL_CLIP, REL_CLIP) + REL_CLIP
    return jnp.transpose(table[idx], (2, 0, 1)).astype(jnp.float32)


def chunk_band_attention(x, cache_k, cache_v, norm_g, w_qkv, q_gain, k_gain, rel_table, w_o):
    bsz, t, _ = x.shape
    past = cache_k.shape[1]
    n_chunks = -(-t // CHUNK)
    t_pad = n_chunks * CHUNK
    h = rms_norm(x, norm_g)
    q, k, v = jnp.split(h @ w_qkv, 3, axis=-1)
    q = rms_norm(q.reshape(bsz, t, N_HEADS, HEAD_DIM), q_gain)
    k = rms_norm(k.reshape(bsz, t, N_HEADS, HEAD_DIM), k_gain)
    v = v.reshape(bsz, t, N_HEADS, HEAD_DIM)
    qp = jnp.pad(q, ((0, 0), (0, t_pad - t), (0, 0), (0, 0)))
    pad_kv = ((0, 0), (LEFT_CTX, t_pad - t), (0, 0), (0, 0))
    kp = jnp.pad(jnp.concatenate([cache_k.astype(k.dtype), k], axis=1), pad_kv)
    vp = jnp.pad(jnp.concatenate([cache_v.astype(v.dtype), v], axis=1), pad_kv)
    bias = rel_bias(rel_table)
    scale = HEAD_DIM ** -0.5
    n_keys = past + t

    def one_chunk(c):
        start = c * CHUNK
        q_c = lax.dynamic_slice_in_dim(qp, start, CHUNK, axis=1)
        k_b = lax.dynamic_slice_in_dim(kp, past + start, BAND, axis=1)
        v_b = lax.dynamic_slice_in_dim(vp, past + start, BAND, axis=1)
        kpos = past + start - LEFT_CTX + jnp.arange(BAND)
        valid = (kpos >= 0) & (kpos < n_keys)
        s = jnp.einsum('bqhd,bkhd->bhqk', q_c, k_b).astype(jnp.float32) * scale + bias
        s = jnp.where(valid, s, NEG_INF)
        p = jax.nn.softmax(s, axis=-1).astype(v_b.dtype)
        return jnp.einsum('bhqk,bkhd->bqhd', p, v_b)

    o = lax.map(one_chunk, jnp.arange(n_chunks))
    o = jnp.moveaxis(o, 0, 1).reshape(bsz, t_pad, D_MODEL)[:, :t]
    return o @ w_o, k, v


def conformer_conv(x, buf, norm_g, w_pw1, b_pw1, w_dw, b_dw, ln_g, ln_b, w_pw2, b_pw2):
    h = rms_norm(x, norm_g)
    a, gate = jnp.split(h @ w_pw1 + b_pw1, 2, axis=-1)
    u = a * jax.nn.sigmoid(gate)
    ext = jnp.concatenate([buf.astype(u.dtype), u], axis=1)
    y = lax.conv_general_dilated(ext, w_dw[:, None, :].astype(ext.dtype), (1,), 'VALID',
                                 dimension_numbers=('NWC', 'WIO', 'NWC'),
                                 feature_group_count=D_MODEL) + b_dw
    y = jax.nn.silu(layer_norm(y, ln_g, ln_b))
    return y @ w_pw2 + b_pw2, ext[:, -(CONV_WIDTH - 1):]


def trunk(x, k_caches, v_caches, conv_bufs, p):
    new_k, new_v, new_conv = [], [], []
    for i in range(DEPTH):
        x = x + 0.5 * swiglu_ffn(x, p['ffn1_norm'][i], p['ffn1_w_gate'][i], p['ffn1_w_up'][i], p['ffn1_w_down'][i])
        if i % 2 == 0:
            a = i // 2
            out, k, v = chunk_band_attention(x, k_caches[a], v_caches[a], p['attn_norm'][a], p['attn_w_qkv'][a],
                                             p['attn_q_gain'][a], p['attn_k_gain'][a], p['attn_rel_bias'][a],
                                             p['attn_w_o'][a])
            new_k.append(k)
            new_v.append(v)
        else:
            b = i // 2
            out, nb = conformer_conv(x, conv_bufs[b], p['conv_norm'][b], p['conv_w_pw1'][b], p['conv_b_pw1'][b],
                                     p['conv_w_dw'][b], p['conv_b_dw'][b], p['conv_ln_g'][b], p['conv_ln_b'][b],
                                     p['conv_w_pw2'][b], p['conv_b_pw2'][b])
            new_conv.append(nb)
        x = x + out
        x = x + 0.5 * swiglu_ffn(x, p['ffn2_norm'][i], p['ffn2_w_gate'][i], p['ffn2_w_up'][i], p['ffn2_w_down'][i])
    return x, jnp.stack(new_k), jnp.stack(new_v), jnp.stack(new_conv)


def setup_inputs(seed: int = 0) -> dict:
    key = jax.random.key(seed)
    ks = iter(jax.random.split(key, 40))
    f32 = jnp.float32

    def nrm(shape, scale):
        return jax.random.normal(next(ks), shape, f32) * scale

    def gain(shape):
        return 1.0 + nrm(shape, 0.05)

    cache_len = min(LEFT_CTX, PAST_LEN)
    A, C = N_ATTN_LAYERS, N_CONV_LAYERS
    return {
        'x_prompt': nrm((BATCH, SEQ, D_MODEL), 1.0),
        'x_sample': nrm((DEC_BATCH, DEC_SEQ, D_MODEL), 1.0),
        'cache_attn_k': nrm((A, DEC_BATCH, cache_len, N_HEADS, HEAD_DIM), 1.0),
        'cache_attn_v': nrm((A, DEC_BATCH, cache_len, N_HEADS, HEAD_DIM), 1.0),
        'state_conv': nrm((C, DEC_BATCH, CONV_WIDTH - 1, D_MODEL), 0.5),
        'ffn1_norm': gain((DEPTH, D_MODEL)),
        'ffn1_w_gate': nrm((DEPTH, D_MODEL, D_FF), D_MODEL ** -0.5),
        'ffn1_w_up': nrm((DEPTH, D_MODEL, D_FF), D_MODEL ** -0.5),
        'ffn1_w_down': nrm((DEPTH, D_FF, D_MODEL), D_FF ** -0.5),
        'ffn2_norm': gain((DEPTH, D_MODEL)),
        'ffn2_w_gate': nrm((DEPTH, D_MODEL, D_FF), D_MODEL ** -0.5),
        'ffn2_w_up': nrm((DEPTH, D_MODEL, D_FF), D_MODEL ** -0.5),
        'ffn2_w_down': nrm((DEPTH, D_FF, D_MODEL), D_FF ** -0.5),
        'attn_norm': gain((A, D_MODEL)),
        'attn_w_qkv': nrm((A, D_MODEL, 3 * D_MODEL), D_MODEL ** -0.5),
        'attn_q_gain': gain((A, HEAD_DIM)),
        'attn_k_gain': gain((A, HEAD_DIM)),
        'attn_rel_bias': nrm((A, N_REL, N_HEADS), 0.5),
        'attn_w_o': nrm((A, D_MODEL, D_MODEL), D_MODEL ** -0.5),
        'conv_norm': gain((C, D_MODEL)),
        'conv_w_pw1': nrm((C, D_MODEL, 2 * D_MODEL), D_MODEL ** -0.5),
        'conv_b_pw1': nrm((C, 2 * D_MODEL), 0.02),
        'conv_w_dw': nrm((C, CONV_WIDTH, D_MODEL), CONV_WIDTH ** -0.5),
        'conv_b_dw': nrm((C, D_MODEL), 0.02),
        'conv_ln_g': gain((C, D_MODEL)),
        'conv_ln_b': nrm((C, D_MODEL), 0.02),
        'conv_w_pw2': nrm((C, D_MODEL, D_MODEL), D_MODEL ** -0.5),
        'conv_b_pw2': nrm((C, D_MODEL), 0.02),
    }


def reference(x_prompt, x_sample, cache_attn_k, cache_attn_v, state_conv,
              ffn1_norm, ffn1_w_gate, ffn1_w_up, ffn1_w_down,
              ffn2_norm, ffn2_w_gate, ffn2_w_up, ffn2_w_down,
              attn_norm, attn_w_qkv, attn_q_gain, attn_k_gain, attn_rel_bias, attn_w_o,
              conv_norm, conv_w_pw1, conv_b_pw1, conv_w_dw, conv_b_dw, conv_ln_g, conv_ln_b,
              conv_w_pw2, conv_b_pw2):
    p = dict(ffn1_norm=ffn1_norm, ffn1_w_gate=ffn1_w_gate, ffn1_w_up=ffn1_w_up, ffn1_w_down=ffn1_w_down,
             ffn2_norm=ffn2_norm, ffn2_w_gate=ffn2_w_gate, ffn2_w_up=ffn2_w_up, ffn2_w_down=ffn2_w_down,
             attn_norm=attn_norm, attn_w_qkv=attn_w_qkv, attn_q_gain=attn_q_gain, attn_k_gain=attn_k_gain,
             attn_rel_bias=attn_rel_bias, attn_w_o=attn_w_o,
             conv_norm=conv_norm, conv_w_pw1=conv_w_pw1, conv_b_pw1=conv_b_pw1, conv_w_dw=conv_w_dw,
             conv_b_dw=conv_b_dw, conv_ln_g=conv_ln_g, conv_ln_b=conv_ln_b, conv_w_pw2=conv_w_pw2,
             conv_b_pw2=conv_b_pw2)
    bp, sp = x_prompt.shape[0], x_prompt.shape[1]
    empty_kv = jnp.zeros((N_ATTN_LAYERS, bp, 0, N_HEADS, HEAD_DIM), x_prompt.dtype)
    zero_conv = jnp.zeros((N_CONV_LAYERS, bp, CONV_WIDTH - 1, D_MODEL), x_prompt.dtype)
    y_prompt, k_p, v_p, new_conv_prompt = trunk(x_prompt, empty_kv, empty_kv, zero_conv, p)
    keep = min(LEFT_CTX, sp)
    new_attn_k_prompt = k_p[:, :, sp - keep:]
    new_attn_v_prompt = v_p[:, :, sp - keep:]
    y_sample, new_attn_k_sample, new_attn_v_sample, new_conv_sample = trunk(
        x_sample, cache_attn_k, cache_attn_v, state_conv, p)
    return (y_prompt, y_sample, new_attn_k_prompt, new_attn_v_prompt,
            new_attn_k_sample, new_attn_v_sample, new_conv_prompt, new_conv_sample)
```

```python
import contextlib
import numpy as np
import concourse.bass as bass
import concourse.mybir as mybir
from concourse.bass_utils import run_bass_kernel_spmd

F32 = mybir.dt.float32
BF16 = mybir.dt.bfloat16
AF = mybir.ActivationFunctionType
ALU = mybir.AluOpType

D = 1024
DFF = 2816
NH = 16
EPS = 1e-6
NEG = -30000.0
NCORES = 8
NT_FULL = 8


def _dsz(dt):
    return 2 if dt == BF16 else 4


class Ins:
    __slots__ = ("eng", "idx", "fn", "deps", "need_inc", "count", "dsem", "dval", "is_dma")

    def __init__(self, eng, idx, fn):
        self.eng = eng
        self.idx = idx
        self.fn = fn
        self.deps = []
        self.need_inc = False
        self.count = 0
        self.dsem = None
        self.dval = 0
        self.is_dma = False


class Prog:
    ENGS = ("pe", "act", "dve", "pool", "sp")

    def __init__(self, nc):
        self.nc = nc
        self.q = {e: [] for e in self.ENGS}
        self.recs = {}
        self.dma_cnt = {}
        self.pool_rr = {}
        self.pool_last = {}
        self.finals = []

    @staticmethod
    def region(ap):
        sp = str(ap.space)
        if "SB" not in sp and "PSUM" not in sp:
            return None
        a = ap.ap
        pstep, pcnt = a[0]
        off = ap.offset
        if pstep > 0:
            p0 = off // pstep
            f0 = off - p0 * pstep
        else:
            p0, f0 = 0, off
        ext = 0
        for st, c in a[1:]:
            ext += abs(st) * (c - 1)
        sz = _dsz(ap.dtype)
        if "PSUM" in sp:
            return (ap.name, 0, 128, 0, 2048)
        return (ap.name, p0, p0 + pcnt, f0 * sz, (f0 + ext + 1) * sz)

    def _access(self, ins, ap, write):
        r = self.region(ap)
        if r is None:
            return
        name, p0, p1, f0, f1 = r
        lst = self.recs.setdefault(name, [])
        keep = []
        for rec in lst:
            rp0, rp1, rf0, rf1, rins, rw = rec
            ov = (rp0 < p1 and p0 < rp1 and rf0 < f1 and f0 < rf1)
            if ov and (write or rw) and rins is not ins:
                ins.deps.append(rins)
            if write and ov and p0 <= rp0 and rp1 <= p1 and f0 <= rf0 and rf1 <= f1:
                continue
            if (not write) and (not rw) and rins.eng == ins.eng and (not rins.is_dma) and (not ins.is_dma) \
                    and rp0 == p0 and rp1 == p1 and rf0 == f0 and rf1 == f1:
                continue
            keep.append(rec)
        keep.append((p0, p1, f0, f1, ins, write))
        self.recs[name] = keep

    def op(self, eng, fn, reads=(), writes=(), dsem=None):
        ins = Ins(eng, len(self.q[eng]), fn)
        if dsem is not None:
            ins.is_dma = True
            ins.dsem = dsem
            self.dma_cnt[dsem] = self.dma_cnt.get(dsem, 0) + 16
            ins.dval = self.dma_cnt[dsem]
            prev = self.pool_last.get(dsem)
            if prev is not None:
                ins.deps.append(prev)
            self.pool_last[dsem] = ins
        for ap in reads:
            self._access(ins, ap, False)
        for ap in writes:
            self._access(ins, ap, True)
        self.q[eng].append(ins)
        return ins


PCOL = {}
_c = 0
for _nm, _w in (("gF", 32), ("gA", 8), ("gC", 8), ("qg", 1), ("kg", 1), ("bfar", 16), ("bpw1", 16),
                ("wdw", 248), ("bdw", 8), ("lng", 8), ("lnb", 8), ("bpw2", 8), ("hval", 1), ("eps", 1)):
    PCOL[_nm] = _c
    _c += _w
NPAR = _c


import os as _os


_OPT = _os.environ.get("KOPT", "a")


def build_program(NT):
    nc = bass.Bass("TRN2", target_bir_lowering=False)
    P = Prog(nc)
    NROWS = 640 + NT * 512

    def din(name, shape):
        return nc.dram_tensor(name, list(shape), F32, kind="ExternalInput").ap()

    def dout(name, shape):
        return nc.dram_tensor(name, list(shape), F32, kind="ExternalOutput").ap()

    xin = din("xin", [NROWS, D])
    ck = din("ck", [512, D])
    cv = din("cv", [512, D])
    sconv = din("sconv", [30, D])
    par_d = din("par", [128, NPAR])
    bias_d = din("biasd", [128, NH * 2 * 128])
    mask_d = din("maskd", [128, 128])
    cst_d = din("cst", [128, 384])
    wg_d = [din("wg1", [2, D, DFF]), din("wg2", [2, D, DFF])]
    wu_d = [din("wu1", [2, D, DFF]), din("wu2", [2, D, DFF])]
    wd_d = [din("wd1", [2, DFF, D]), din("wd2", [2, DFF, D])]
    wqkv_d = din("wqkv", [D, 3 * D])
    wo_d = din("wo", [D, D])
    wpw1_d = din("wpw1", [D, 2 * D])
    wpw2_d = din("wpw2", [D, D])

    y_own = dout("y_own", [NT * 512, D])
    y_s = dout("y_s", [64, D])
    k_last = dout("o_klast", [512, D])
    v_last = dout("o_vlast", [512, D])
    k_s = dout("o_ks", [64, D])
    v_s = dout("o_vs", [64, D])
    conv_last = dout("conv_last", [30, D])
    conv_s = dout("conv_s", [30, D])

    es = contextlib.ExitStack()

    def sb(name, n, dt):
        return es.enter_context(nc.sbuf_tensor(name, [128, n], dt))

    with es:
        xT_A = sb("xT_A", 8 * 512, F32)
        xT_B = sb("xT_B", 8 * 128, F32)
        hT_A = sb("hT_A", 8 * 512, BF16)
        hT_B = sb("hT_B", 8 * 128, BF16)
        R1 = sb("R1", 22 * 512 // 2, F32)
        AT_B = sb("AT_B", 22 * 128, BF16)
        qT_B = sb("qT_B", 8 * 128, BF16)
        kwin = sb("kwin", 8 * 1152, BF16)
        vwin = sb("vwin", 9 * 1040, BF16)
        RS = sb("RS", 5184, F32)
        wsl = [sb("wsl%d" % i, 3072, BF16) for i in range(4)]
        sq = sb("sq", 8 * 512, BF16)
        rstd = sb("rstd", 512, F32)
        sg = [sb("sg%d" % i, 512, F32) for i in range(2)]
        xst = sb("xst", 1024, F32)
        ost = [sb("ost%d" % i, 1024, F32) for i in range(2)]
        biasT = sb("biasT", NH * 2 * 128, F32)
        maskT = sb("maskT", 128, F32)
        par = sb("par_sb", NPAR, F32)
        cst = sb("cst_sb", 384, F32)
        identb = sb("identb", 128, BF16)
        onesb = sb("onesb", 128, BF16)
        blkb = sb("blkb", 128, BF16)
        tmpq = sb("tmpq", 512, BF16)
        lnm = sb("lnm", 512, F32)
        lnr = sb("lnr", 512, F32)
        lnt = [sb("lnt%d" % i, 512, F32) for i in range(2)]
        tmpq2 = sb("tmpq2", 512, BF16)
        kn32s, rs2s, tmpqs = [lnt[0], lnm], [lnt[1], lnr], [tmpq, tmpq2]
        rden = sb("rden", 16, F32)
        chist = sb("chist", 8 * 30, F32)
        ebf_t = sb("ebf", 8 * 542, BF16)
        dgs = [AT_B[:, 2048 + i * 128: 2048 + (i + 1) * 128] for i in range(6)]
        ps = [es.enter_context(nc.psum_tensor("ps%d" % i, [128, 512], F32)) for i in range(8)]

        sems = {e: es.enter_context(nc.semaphore("s_" + e)) for e in ("pe", "act", "dve", "pool")}
        wsem = [es.enter_context(nc.semaphore("w%d" % i)) for i in range(4)]
        gsem = [es.enter_context(nc.semaphore("g%d" % i)) for i in range(8)]
        gs_i = [0]

        def next_gsem():
            s = gsem[gs_i[0] % len(gsem)]
            gs_i[0] += 1
            return s

        def v3(t, a, b):
            return t[:].rearrange("p (a b) -> p a b", a=a, b=b)

        xA = v3(xT_A, 8, 512)
        xB = v3(xT_B, 8, 128)
        hA = v3(hT_A, 8, 512)
        hB = v3(hT_B, 8, 128)
        R1b = R1[:].bitcast(BF16)
        ATA = R1b.rearrange("p (a b) -> p a b", a=22, b=512)
        ATB = v3(AT_B, 22, 128)
        qA = R1b[:, 0:4096].rearrange("p (a b) -> p a b", a=8, b=512)
        osb = R1b[:, 4096:5120]
        PT = [R1b[:, 5120 + i * 640: 5120 + (i + 1) * 640].rearrange("p (a b) -> p a b", a=5, b=128)
              for i in range(3)]
        snf = [R1[:, 3520 + i * 384: 3520 + (i + 1) * 384].rearrange("p (a b) -> p a b", a=3, b=128)
               for i in range(2)]
        ext = R1[:, 0:8 * 542].rearrange("p (a b) -> p a b", a=8, b=542)
        qB = v3(qT_B, 8, 128)
        kw = v3(kwin, 8, 1152)
        vw = vwin[:].rearrange("p (t h e) -> p t h e", t=9, h=16, e=65)
        RSb = RS[:].bitcast(BF16)
        ks = RSb[:, 0:5120].rearrange("p (a b) -> p a b", a=8, b=640)
        vs = RS[:, 2560:5160].bitcast(BF16).rearrange("p (t h e) -> p t h e", t=5, h=16, e=65)
        yc = RS[:, 0:4096].rearrange("p (a b) -> p a b", a=8, b=512)
        exs = RS[:, 2048:2048 + 752].rearrange("p (a b) -> p a b", a=8, b=94)
        uh = RS[:, 0:1024].rearrange("p (a b) -> p a b", a=8, b=128)
        ycs = RS[:, 1024:1536].rearrange("p (a b) -> p a b", a=8, b=64)
        chv = v3(chist, 8, 30)
        ebf = v3(ebf_t, 8, 542)
        ebfs = RS[:, 1536:1536 + 376].bitcast(BF16).rearrange("p (a b) -> p a b", a=8, b=94)
        bT = biasT[:].rearrange("p (h t q) -> p h t q", h=NH, t=2, q=128)
        identf = cst[:, 0:128]
        pc = lambda nm, i=0, n=1: par[:, PCOL[nm] + i: PCOL[nm] + i + n]
        wdw = par[:, PCOL["wdw"]:PCOL["wdw"] + 248].rearrange("p (k t) -> p k t", k=8, t=31)

        def mm(out, lhsT, rhs, start, stop):
            P.op("pe", lambda e: e.matmul(out, lhsT=lhsT, rhs=rhs, start=start, stop=stop),
                 reads=[lhsT, rhs], writes=[out])

        def tr(out, in_, ident):
            P.op("pe", lambda e: e.transpose(out, in_, ident), reads=[in_, ident], writes=[out])

        def act(out, in_, func, bias=None, scale=None):
            rd = [in_]
            kw_ = {}
            if bias is not None:
                kw_["bias"] = bias
                if not isinstance(bias, float):
                    rd.append(bias)
            if scale is not None:
                kw_["scale"] = scale
                if not isinstance(scale, float):
                    rd.append(scale)
            P.op("act", lambda e: e.activation(out=out, in_=in_, func=func, **kw_), reads=rd, writes=[out])

        def stt(eng, out, in0, scalar, in1, op0, op1):
            rd = [in0, in1] + ([] if isinstance(scalar, float) else [scalar])
            P.op(eng, lambda e: e.scalar_tensor_tensor(out=out, in0=in0, scalar=scalar, in1=in1, op0=op0, op1=op1),
                 reads=rd, writes=[out])

        def tsc(eng, out, in0, s1, s2, op0, op1=None):
            rd = [in0] + [s for s in (s1, s2) if s is not None and not isinstance(s, float)]
            if op1 is None:
                P.op(eng, lambda e: e.tensor_scalar(out=out, in0=in0, scalar1=s1, scalar2=None, op0=op0),
                     reads=rd, writes=[out])
            else:
                P.op(eng, lambda e: e.tensor_scalar(out=out, in0=in0, scalar1=s1, scalar2=s2, op0=op0, op1=op1),
                     reads=rd, writes=[out])

        def tt(eng, out, in0, in1, op):
            P.op(eng, lambda e: e.tensor_tensor(out=out, in0=in0, in1=in1, op=op), reads=[in0, in1], writes=[out])

        def cp(eng, out, in_):
            if eng == "act":
                P.op("act", lambda e: e.copy(out=out, in_=in_), reads=[in_], writes=[out])
            else:
                P.op(eng, lambda e: e.tensor_copy(out=out, in_=in_), reads=[in_], writes=[out])

        def recip(out, in_):
            P.op("dve", lambda e: e.reciprocal(out=out, in_=in_), reads=[in_], writes=[out])

        def rsqrt_eps(out, src):
            act(out, src, AF.Ln, bias=pc("eps"))
            act(out, out, AF.Exp, scale=-0.5)

        def mset(eng, ap, val):
            P.op(eng, lambda e: e.memset(ap, val), writes=[ap])

        def dma(q, out, in_, sem=None):
            s = sem if sem is not None else next_gsem()
            return P.op(q, lambda e: e.dma_start(out=out, in_=in_), reads=[in_], writes=[out], dsem=s)

        ws_i = [0]

        def wload(src3, kch, ncols):
            i = ws_i[0] % 4
            ws_i[0] += 1
            dst = wsl[i][:, 0:kch * ncols].rearrange("p (k f) -> p k f", k=kch, f=ncols)
            dma("pool", dst, src3, sem=wsem[i])
            return dst

        dma("sp", par[:], par_d[:, :])
        dma("sp", cst[:], cst_d[:, :])
        dma("sp", biasT[:], bias_d[:, :])
        dma("sp", maskT[:], mask_d[:, :])
        cp("dve", identb[:], cst[:, 0:128])
        cp("dve", onesb[:], cst[:, 128:256])
        cp("dve", blkb[:], cst[:, 256:384])
        for h in range(NH):
            tsc("dve", bT[:, h], bT[:, h], pc("bfar", h), None, ALU.subtract)
        mset("pool", kwin[:], 0.0)
        mset("pool", vwin[:], 0.0)

        class Seg:
            def __init__(self, x, h, at, w, lo=0):
                self.x, self.h, self.at, self.w, self.lo = x, h, at, w, lo

            def sub(self, lo, w):
                return Seg(self.x, self.h, self.at, w, self.lo + lo)

            def X(self, k):
                return self.x[:, k, self.lo:self.lo + self.w]

            def H(self, k):
                return self.h[:, k, self.lo:self.lo + self.w]

            def A(self, j):
                return self.at[:, j, self.lo:self.lo + self.w]

        psi = {"g": 0, "u": 0, "d": 0}

        tbk = [0]

        def tbank():
            b = ps[4 + tbk[0] % 4]
            tbk[0] += 1
            return b

        def prefetch_x(row0):
            for r in range(4):
                dma("sp", RS[:, r * 1024:(r + 1) * 1024], xin[row0 + r * 128: row0 + (r + 1) * 128, :])

        def load_x(seg, row0, pre=False):
            for r in range(seg.w // 128):
                if pre:
                    src = RS[:, r * 1024:(r + 1) * 1024]
                else:
                    src = xst[:]
                    dma("sp", xst[:], xin[row0 + r * 128: row0 + (r + 1) * 128, :])
                for half in range(2):
                    bank = tbank()
                    for kk in range(4):
                        k = half * 4 + kk
                        tr(bank[:, kk * 128:(kk + 1) * 128], src[:, k * 128:(k + 1) * 128], identf)
                    cp("act" if half == 0 else "dve",
                       seg.x[:, half * 4:half * 4 + 4, seg.lo + r * 128: seg.lo + (r + 1) * 128],
                       bank[:].rearrange("p (k t) -> p k t", k=4, t=128))

        pre_stat = {}

        def stat_partial(s, k):
            w = s.w
            sqk = sq[:].rearrange("p (k t) -> p k t", k=8, t=512)[:, k, 0:w]
            act(sqk, s.X(k), AF.Square)
            if k >= 1:
                sqp = sq[:].rearrange("p (k t) -> p k t", k=8, t=512)[:, k - 1, 0:w]
                mm(ps[6][:, 0:w], onesb[:], sqp, k == 1, False)
            if k == 7:
                pre_stat[(id(s.x), s.lo, s.w)] = True

        def rmsnorm(segs, gcol):
            for s in segs:
                w = s.w
                if pre_stat.pop((id(s.x), s.lo, s.w), False) and len(segs) == 1:
                    sq7 = sq[:].rearrange("p (k t) -> p k t", k=8, t=512)[:, 7, 0:w]
                    mm(ps[6][:, 0:w], onesb[:], sq7, False, True)
                else:
                    sqv = sq[:].rearrange("p (k t) -> p k t", k=8, t=512)[:, :, 0:w]
                    act(sqv, s.x[:, :, s.lo:s.lo + w], AF.Square)
                    for k in range(8):
                        mm(ps[6][:, 0:w], onesb[:], sqv[:, k, :], k == 0, k == 7)
                rsqrt_eps(rstd[:, 0:w], ps[6][:, 0:w])
                for k in range(8):
                    stt("dve", s.H(k), s.X(k), par[:, gcol + k: gcol + k + 1],
                        rstd[:, 0:w], ALU.mult, ALU.mult)

        def wsrc(w2d, kch, c0, ncols):
            return w2d[:, c0:c0 + ncols].rearrange("(k p) f -> p k f", p=128)

        def ffn(segs, fi):
            li, which = fi // 2, fi % 2
            rmsnorm(segs, PCOL["gF"] + fi * 8)
            Wg, Wu, Wd = wg_d[which][li], wu_d[which][li], wd_d[which][li]
            for half in range(2):
                j0 = half * 11
                for g0, gn in ((0, 3), (3, 3), (6, 3), (9, 2)):
                    wgt = wload(wsrc(Wg, 8, (j0 + g0) * 128, gn * 128), 8, gn * 128)
                    wut = wload(wsrc(Wu, 8, (j0 + g0) * 128, gn * 128), 8, gn * 128)
                    for n in range(gn):
                        j = g0 + n
                        for s in segs:
                            w = s.w
                            gb = ps[psi["g"] % 2]
                            ub = ps[2 + psi["g"] % 2]
                            sgt = sg[psi["g"] % 2]
                            psi["g"] += 1
                            for k in range(8):
                                mm(gb[:, 0:w], wgt[:, k, n * 128:(n + 1) * 128], s.H(k), k == 0, k == 7)
                            for k in range(8):
                                mm(ub[:, 0:w], wut[:, k, n * 128:(n + 1) * 128], s.H(k), k == 0, k == 7)
                            act(sgt[:, 0:w], gb[:, 0:w], AF.Silu)
                            tt("dve", s.A(j), sgt[:, 0:w], ub[:, 0:w], ALU.mult)
                for n0 in range(0, 8, 2):
                    wdt = wload(Wd[j0 * 128:(j0 + 11) * 128, n0 * 128:(n0 + 2) * 128]
                                .rearrange("(k p) f -> p k f", p=128), 11, 256)
                    for n in range(2):
                        for s in segs:
                            w = s.w
                            db = ps[4 + psi["d"] % 2]
                            psi["d"] += 1
                            for j in range(11):
                                mm(db[:, 0:w], wdt[:, j, n * 128:(n + 1) * 128], s.A(j), j == 0, j == 10)
                            stt("dve", s.X(n0 + n), db[:, 0:w], 0.5, s.X(n0 + n), ALU.mult, ALU.add)
                            if half == 1 and fi != 3 and len(segs) == 1 and s.w == 512:
                                stat_partial(s, n0 + n)

        def linear8(segs, W2d, c0, nout, evac):
            for g0 in range(0, nout, 3):
                gn = min(3, nout - g0)
                wt = wload(wsrc(W2d, 8, c0 + g0 * 128, gn * 128), 8, gn * 128)
                for n in range(gn):
                    for s in segs:
                        b = ps[psi["g"] % 4]
                        psi["g"] += 1
                        for k in range(8):
                            mm(b[:, 0:s.w], wt[:, k, n * 128:(n + 1) * 128], s.H(k), k == 0, k == 7)
                        evac(g0 + n, s, b[:, 0:s.w])

        def out_rows(dst_rows, src_fn, ncol, eng_i=[0]):
            o = ost[eng_i[0] % 2]
            eng_i[0] += 1
            for half in range(2):
                bank = tbank()
                for kk in range(4):
                    tr(bank[0:ncol, kk * 128:(kk + 1) * 128], src_fn(half * 4 + kk), identf)
                cp("act" if half == 0 else "dve", o[0:ncol, half * 512:(half + 1) * 512], bank[0:ncol, :])
            d = dma("sp", dst_rows, o[0:ncol, :])
            P.finals.append(d)

        def qkv(segs, kdst, vdst, kout, vout, onescol):
            rmsnorm(segs, PCOL["gA"])

            pend = []
            qcnt = [0]

            def evac_qk(n, s, pb):
                i = qcnt[0]
                qcnt[0] += 1
                act(tmpqs[i % 2][:, 0:s.w], pb, AF.Square)
                if pend:
                    evac_qk2(*pend.pop())
                pend.append((i, n, s, pb))

            def evac_qk2(i, n, s, pb):
                w = s.w
                isk = n >= 8
                hp = n % 8
                tmpq_, rs2, kn32 = tmpqs[i % 2], rs2s[i % 2], kn32s[i % 2]
                mm(ps[4 + i % 2][:, 0:w], blkb[:], tmpq_[:, 0:w], True, True)
                rsqrt_eps(rs2[:, 0:w], ps[4 + i % 2][:, 0:w])
                gcol = pc("kg") if isk else pc("qg")
                if not isk:
                    stt("dve", s.qdst[:, hp, s.lo:s.lo + w], pb, gcol, rs2[:, 0:w], ALU.mult, ALU.mult)
                else:
                    stt("dve", kn32[:, 0:w], pb, gcol, rs2[:, 0:w], ALU.mult, ALU.mult)
                    for (dst, lo, hi) in kdst(s, hp):
                        cp("dve" if "c" in _OPT else "act", dst, kn32[:, lo:hi])
                    ko = kout(s) if "k" not in _os.environ.get("KSKIP", "") else None
                    if ko is not None:
                        rows, lo, ncol, kst = ko
                        for ci, c in enumerate(range(0, ncol, 128)):
                            cw = min(128, ncol - c)
                            tr(ps[7][0:cw, 0:128], kn32[:, lo + c: lo + c + cw], identf)
                            cp("act", kst[ci][0:cw, hp * 128:(hp + 1) * 128], ps[7][0:cw, 0:128])
                            if hp == 7:
                                d = dma("sp", rows[c:c + cw, :], kst[ci][0:cw, :])
                                P.finals.append(d)

            linear8(segs, wqkv_d, 0, 16, evac_qk)
            evac_qk2(*pend.pop())
            for c0, ncg in ((0, 384), (384, 384), (768, 256)):
                wt = wload(wsrc(wqkv_d, 8, 2048 + c0, ncg), 8, ncg)
                h0, nhg = c0 // 64, ncg // 64
                for s in segs:
                    for r in range(s.w // 128):
                        b = ps[psi["g"] % 4]
                        psi["g"] += 1
                        for k in range(8):
                            mm(b[:, 0:ncg], s.h[:, k, s.lo + r * 128: s.lo + (r + 1) * 128], wt[:, k, :], k == 0, k == 7)
                        bv = b[:, 0:ncg].rearrange("p (h e) -> p h e", h=nhg, e=64)
                        for (dst, lo, hi) in vdst(s, r):
                            cp("act" if (r % 2 == 0 or "c" not in _OPT) else "dve", dst[lo:hi, h0:h0 + nhg, 0:64], bv[lo:hi])
                        vo = vout(s, r) if "v" not in _os.environ.get("KSKIP", "") else None
                        if vo is not None:
                            rows, lo, nrow, vst = vo
                            cp("act", vst[lo:lo + nrow, c0:c0 + ncg], b[lo:lo + nrow, 0:ncg])
                            if c0 == 768 and "d" not in _os.environ.get("KSKIP", ""):
                                d = dma("sp", rows[:, :], vst[lo:lo + nrow, :])
                                P.finals.append(d)
            onescol()

        pti = [0]

        def attn_pair(kT, vT, t0, qsrc, q0, nq, odst, o0, prev_tail=None):
            obanks = (ps[4], ps[5], ps[6])
            pts = {}

            def stage1(h):
                hp, po = h // 2, (h % 2) * 64
                nb = ps[h % 2]
                fb = ps[2 + h % 2]
                q = qsrc[po:po + 64, hp, q0:q0 + nq]
                for i, j in enumerate((0, 3, 4)):
                    mm(nb[:, i * 128:i * 128 + nq], kT[po:po + 64, hp, (t0 + j) * 128:(t0 + j + 1) * 128], q, True, True)
                for i, j in enumerate((1, 2)):
                    mm(fb[:, i * 128:i * 128 + nq], kT[po:po + 64, hp, (t0 + j) * 128:(t0 + j + 1) * 128], q, True, True)
                pt = PT[pti[0] % 3]
                sn = snf[pti[0] % 2]
                pti[0] += 1
                pts[h] = pt
                nbv = nb[:, 0:384].rearrange("p (a b) -> p a b", a=3, b=128)
                fbv = fb[:, 0:256].rearrange("p (a b) -> p a b", a=2, b=128)
                stt("dve", sn[:, 0, 0:nq], nbv[:, 0, 0:nq], 0.125, maskT[:, 0:nq], ALU.mult, ALU.add)
                stt("dve", sn[:, 1:3, 0:nq], nbv[:, 1:3, 0:nq], 0.125, bT[:, h, :, 0:nq], ALU.mult, ALU.add)
                act(pt[:, 0:3, 0:nq], sn[:, :, 0:nq], AF.Exp)
                act(pt[:, 3:5, 0:nq], fbv[:, :, 0:nq], AF.Exp, scale=0.125)

            def stage2(h):
                pt = pts[h]
                ob = obanks[h // 7]
                oc = (h % 7) * 65
                for i, j in enumerate((0, 3, 4, 1, 2)):
                    mm(ob[0:nq, oc:oc + 65], pt[:, i, 0:nq], vT[:, t0 + j, h, :], i == 0, i == 4)

            stage1(0)
            for h in range(NH):
                if h + 1 < NH:
                    stage1(h + 1)
                if h == 0 and prev_tail is not None:
                    prev_tail()
                stage2(h)
            for g, (h0, nh) in enumerate(((0, 7), (7, 7), (14, 2))):
                ov = obanks[g][0:nq, 0:nh * 65].rearrange("p (h e) -> p h e", h=nh, e=65)
                P.op("dve", lambda e, ov=ov, h0=h0, nh=nh: e.reciprocal(out=rden[0:nq, h0:h0 + nh], in_=ov[:, :, 64]),
                     reads=[ov[:, :, 64]], writes=[rden[0:nq, h0:h0 + nh]])
                tt("dve", osb[0:nq, h0 * 64:(h0 + nh) * 64].rearrange("p (h e) -> p h e", h=nh, e=64),
                   ov[:, :, 0:64], rden[0:nq, h0:h0 + nh].unsqueeze(2).to_broadcast([nq, nh, 64]), ALU.mult)
            def tail():
                tb = ps[7][:].bitcast(BF16)
                for k in range(8):
                    tr(tb[:, k * 128:k * 128 + nq], osb[0:nq, k * 128:(k + 1) * 128], identb[0:nq, 0:nq])
                cp("act", odst[:, :, o0:o0 + nq], tb.rearrange("p (k t) -> p k t", k=8, t=128)[:, :, 0:nq])
            return tail

        def wo_proj(segs):
            def ev(n, s, pb):
                tt("dve", s.X(n), pb, s.X(n), ALU.add)
                if len(segs) == 1 and s.w == 512:
                    stat_partial(s, n)
            linear8(segs, wo_d, 0, 8, ev)

        def pw1_glu(segs, udst):
            rmsnorm(segs, PCOL["gC"])
            for g0, gn in ((0, 3), (3, 3), (6, 2)):
                wa = wload(wsrc(wpw1_d, 8, g0 * 128, gn * 128), 8, gn * 128)
                wgt = wload(wsrc(wpw1_d, 8, 1024 + g0 * 128, gn * 128), 8, gn * 128)
                for n in range(gn):
                    for s in segs:
                        w = s.w
                        ab = ps[psi["g"] % 2]
                        gb = ps[2 + psi["g"] % 2]
                        sgt = sg[psi["g"] % 2]
                        psi["g"] += 1
                        for k in range(8):
                            mm(ab[:, 0:w], wa[:, k, n * 128:(n + 1) * 128], s.H(k), k == 0, k == 7)
                        for k in range(8):
                            mm(gb[:, 0:w], wgt[:, k, n * 128:(n + 1) * 128], s.H(k), k == 0, k == 7)
                        act(sgt[:, 0:w], gb[:, 0:w], AF.Sigmoid, bias=pc("bpw1", 8 + g0 + n))
                        stt("dve", udst(s, g0 + n), ab[:, 0:w], pc("bpw1", g0 + n), sgt[:, 0:w], ALU.add, ALU.mult)

        dgi = [0]

        def dwconv_ln(e3, eb3, ycv, w, hdst):
            cp("act", eb3[:, :, :], e3[:, :, :])
            sqv = sq[:].rearrange("p (k t) -> p k t", k=8, t=512)[:, :, 0:w]

            def ln_stats(k):
                mm(ps[6][:, 0:w], onesb[:], tmpqs[k % 2][:, 0:w], k == 0, k == 7)
                mm(ps[7][:, 0:w], onesb[:], sqv[:, k, :], k == 0, k == 7)

            for k in range(8):
                cb = ps[k % 4]
                for t in range(31):
                    dg = dgs[dgi[0] % 6]
                    dgi[0] += 1
                    if t % 2 == 0 or "b" in _OPT:
                        tsc("dve", dg, identb[:], wdw[:, k, t:t + 1], None, ALU.mult)
                    else:
                        act(dg, identb[:], AF.Copy, scale=wdw[:, k, t:t + 1])
                    mm(cb[:, 0:w], dg, eb3[:, k, t:t + w], t == 0, t == 30)
                tsc("dve", ycv[:, k, 0:w], cb[:, 0:w], pc("bdw", k), None, ALU.add)
                cp("act", tmpqs[k % 2][:, 0:w], ycv[:, k, 0:w])
                act(sqv[:, k, :], ycv[:, k, 0:w], AF.Square)
                if k >= 1:
                    ln_stats(k - 1)
            ln_stats(7)
            cp("dve", lnm[:, 0:w], ps[6][:, 0:w])
            tt("dve", lnr[:, 0:w], lnm[:, 0:w], lnm[:, 0:w], ALU.mult)
            tt("dve", lnr[:, 0:w], ps[7][:, 0:w], lnr[:, 0:w], ALU.subtract)
            tsc("dve", lnr[:, 0:w], lnr[:, 0:w], 0.0, EPS, ALU.max, ALU.add)
            act(lnr[:, 0:w], lnr[:, 0:w], AF.Ln)
            act(lnr[:, 0:w], lnr[:, 0:w], AF.Exp, scale=-0.5)
            for k in range(8):
                t_ = lnt[k % 2]
                tt("dve", t_[:, 0:w], ycv[:, k, 0:w], lnm[:, 0:w], ALU.subtract)
                tt("dve", t_[:, 0:w], t_[:, 0:w], lnr[:, 0:w], ALU.mult)
                act(hdst(k), t_[:, 0:w], AF.Silu, bias=pc("lnb", k), scale=pc("lng", k))

        def pw2_proj(segs):
            def ev(n, s, pb):
                stt("dve", s.X(n), pb, pc("bpw2", n), s.X(n), ALU.add, ALU.add)
                if len(segs) == 1 and s.w == 512:
                    stat_partial(s, n)
            linear8(segs, wpw2_d, 0, 8, ev)

        _STOP = int(_os.environ.get("KSTOP", "0"))

        class _Stop(Exception):
            pass

        def chk(n):
            if _STOP == n:
                raise _Stop()

        try:
            A = Seg(xA, hA, ATA, 512)
            A.qdst = qA
            B = Seg(xB, hB, ATB, 128)
            B.qdst = qB
            load_x(B, 0)
            load_x(A, 128)
            chk(1)
            ffn([A, B], 0)
            chk(2)

            hv = pc("hval")

            def kdst_H(s, hp):
                if s is A:
                    return [(kw[:, hp, 128:640], 0, 512)]
                return [(ks[:, hp, 512:576], 0, 64), (kw[:, hp, 64:128], 64, 128)]

            def vdst_H(s, r):
                if s is A:
                    return [(vw[:, 1 + r], 0, 128)]
                return [(vs[:, 4], 0, 64), (vw[:, 0], 64, 128)]

            def ones_H():
                mset("dve", vw[:, 1:5, :, 64], 1.0)
                mset("dve", vw[64:128, 0, :, 64], 1.0)

            mset("dve", RS[:], 0.0)
            for t in range(4 if "c" not in _os.environ.get("KSKIP", "") else 0):
                dma("sp", ost[t % 2][:], cv[t * 128:(t + 1) * 128, :])
                cp("dve", vs[:, t, :, 0:64], ost[t % 2][:].rearrange("p (h e) -> p h e", h=16, e=64))
                mset("dve", vs[:, t, :, 64], 1.0)
            mset("dve", vs[0:64, 4, :, 64], 1.0)
            for t in range(4 if "c" not in _os.environ.get("KSKIP", "") else 0):
                dma("sp", xst[:], ck[t * 128:(t + 1) * 128, :])
                for half in range(2):
                    bank = ps[6 + half]
                    for kk in range(4):
                        k = half * 4 + kk
                        tr(bank[:, kk * 128:(kk + 1) * 128], xst[:, k * 128:(k + 1) * 128], identf)
                    cp("act" if half == 0 else "dve", ks[:, half * 4:half * 4 + 4, t * 128:(t + 1) * 128],
                       bank[:].rearrange("p (k t) -> p k t", k=4, t=128))

            qkv([A, B], kdst_H, vdst_H,
                lambda s: (k_s, 0, 64, [ost[0][:]]) if s is B else None,
                lambda s, r: (v_s, 0, 64, xst[:]) if s is B else None,
                ones_H)
            chk(3)
            tl = attn_pair(kw, vw, 0, qA, 384, 128, hA, 384)
            for t in range(1, 5):
                cp("dve", vw[:, t, :, 64], hv.to_broadcast([128, 16]))
            tl = attn_pair(ks, vs, 0, qB, 0, 64, hB, 0, prev_tail=tl)
            tl()
            chk(4)
            cp("dve", xB[:, :, 64:128], xA[:, :, 448:512])
            cp("act", hB[:, :, 64:128], hA[:, :, 448:512])
            if "a" in _OPT:
                prefetch_x(640)
            B2 = B.sub(0, 64)
            chk(7)

            A.qdst = qA
            for ti in range(NT):
                last = ti == NT - 1
                load_x(A, 640 + ti * 512, pre=("a" in _OPT))
                ffn([A], 0)
                qkv([A],
                    lambda s, hp: [(kw[:, hp, 640:1152], 0, 512)],
                    lambda s, r: [(vw[:, 5 + r], 0, 128)],
                    (lambda s: (k_last, 0, 512, [xT_B[:], AT_B[:].bitcast(F32)[:, 0:1024],
                                                 sq[:].bitcast(F32)[:, 0:1024], sq[:].bitcast(F32)[:, 1024:2048]]))
                    if last else (lambda s: None),
                    (lambda s, r: (v_last[r * 128:(r + 1) * 128, :], 0, 128, RS[:, r * 1024:(r + 1) * 1024]))
                    if last else (lambda s, r: None),
                    lambda: mset("dve", vw[:, 5:9, :, 64], 1.0))
                tl = None
                for p in range(4):
                    tl = attn_pair(kw, vw, 1 + p, qA, p * 128, 128, hA, p * 128, prev_tail=tl)
                tl()
                cp("act", kw[:, :, 128:640], kw[:, :, 640:1152])
                cp("dve", vw[:, 1:5], vw[:, 5:9])
                first = ti == 0
                post = [A, B] if first else [A]
                wo_proj(post)
                ffn(post, 1)
                ffn(post, 2)
                if first:
                    dma("sp", xst[0:30, :], sconv[:, :])
                    for half in range(2):
                        bank = ps[6 + half]
                        for kk in range(4):
                            k = half * 4 + kk
                            tr(bank[:, kk * 128:kk * 128 + 30], xst[0:30, k * 128:(k + 1) * 128], identf[0:30, 0:30])
                        cp("act" if half == 0 else "dve", exs[:, half * 4:half * 4 + 4, 0:30],
                           bank[:].rearrange("p (k t) -> p k t", k=4, t=128)[:, :, 0:30])
                pw1_glu(post, lambda s, n: ext[:, n, 30:542] if s is A else uh[:, n, :])
                if first:
                    tsc("dve", chv, uh[:, :, 98:128], hv, None, ALU.mult)
                    cp("dve", exs[:, :, 30:94], uh[:, :, 0:64])
                    out_rows(conv_s[:, :], lambda k: exs[:, k, 64:94], 30)
                    dwconv_ln(exs, ebfs, ycs, 64, lambda k: hB[:, k, 0:64])
                cp("dve", ext[:, :, 0:30], chv)
                if last:
                    out_rows(conv_last[:, :], lambda k: ext[:, k, 512:542], 30)
                dwconv_ln(ext, ebf, yc, 512, lambda k: hA[:, k, :])
                if not last and "a" in _OPT:
                    prefetch_x(640 + (ti + 1) * 512)
                cp("dve", chv, ext[:, :, 512:542])
                post2 = [A, B2] if first else [A]
                pw2_proj(post2)
                ffn(post2, 3)
                if first:
                    out_rows(y_s[:, :], lambda k: xB[:, k, 0:64], 64)
                for r in range(4):
                    out_rows(y_own[ti * 512 + r * 128: ti * 512 + (r + 1) * 128, :],
                             lambda k, r=r: xA[:, k, r * 128:(r + 1) * 128], 128)

        except _Stop:
            pass
        with nc.Block() as block:
            @block.tensor
            def _(e):
                _emit_engine(P, "pe", e, sems)

            @block.scalar
            def _(e):
                _emit_engine(P, "act", e, sems)

            @block.vector
            def _(e):
                _emit_engine(P, "dve", e, sems)

            @block.gpsimd
            def _(e):
                _emit_engine(P, "pool", e, sems)

            @block.sync
            def _(e):
                _emit_engine(P, "sp", e, sems)
    return nc


_PREP = {}


def _emit_engine(P, eng, h, sems):
    if id(P) not in _PREP:
        _PREP.clear()
        _PREP[id(P)] = True
        for e in P.ENGS:
            for c in P.q[e]:
                real = []
                for p in c.deps:
                    if p.is_dma:
                        real.append(p)
                    elif p.eng == c.eng:
                        if c.is_dma or c.eng != "pe":
                            real.append(p)
                            p.need_inc = True
                    else:
                        real.append(p)
                        p.need_inc = True
                c.deps = real
        for e in P.ENGS:
            n = 0
            for c in P.q[e]:
                if c.need_inc and not c.is_dma:
                    n += 1
                    c.count = n
    waited = {}
    for c in P.q[eng]:
        need = {}
        for p in c.deps:
            if p.is_dma:
                k, v = p.dsem, p.dval
            else:
                k, v = sems[p.eng], p.count
            if v > need.get(k, 0):
                need[k] = v
        for k, v in need.items():
            if v > waited.get(k, 0):
                h.wait_ge(k, v)
                waited[k] = v
        bi = c.fn(h)
        if c.is_dma:
            bi.then_inc(c.dsem, 16)
        elif c.need_inc:
            bi.then_inc(sems[eng], 1)
    if eng == "sp":
        for p in P.finals:
            if p.dval > waited.get(p.dsem, 0):
                h.wait_ge(p.dsem, p.dval)
                waited[p.dsem] = p.dval


def _chunked(v):
    v = np.asarray(v, np.float32).reshape(-1, 128)
    return np.ascontiguousarray(v.T)


def _bias_tiles(table):
    kk = np.arange(128)[:, None]
    qq = np.arange(128)[None, :]
    out = np.empty((128, NH, 2, 128), np.float32)
    for t, j in enumerate((3, 4)):
        dist = 512 + qq - (j * 128 + kk)
        qc = qq // 64
        m = (j * 128 + kk) // 64
        valid = (m >= qc) & (m <= qc + 8)
        idx = np.clip(dist, -128, 128) + 128
        vals = table[idx]
        vals = np.where(valid[:, :, None], vals, np.float32(NEG))
        out[:, :, t, :] = np.transpose(vals, (0, 2, 1))
    m0 = kk // 64
    qc = qq // 64
    maskd = np.where((m0 >= qc), np.float32(0.0), np.float32(NEG)).astype(np.float32)
    maskd = np.broadcast_to(maskd, (128, 128)).copy()
    return out.reshape(128, NH * 2 * 128), maskd


def _run(inputs, NT):
    f = lambda k: np.asarray(inputs[k], np.float32)
    x_prompt, x_sample = f("x_prompt"), f("x_sample")
    ck_all, cv_all, sc_all = f("cache_attn_k"), f("cache_attn_v"), f("state_conv")
    B, S, _ = x_prompt.shape
    per = NT * 512
    cps = S // per
    assert B * cps == NCORES and x_sample.shape[0] == NCORES

    par = np.zeros((128, NPAR), np.float32)

    def put(nm, arr):
        arr = np.asarray(arr, np.float32)
        par[:, PCOL[nm]:PCOL[nm] + arr.shape[1]] = arr

    put("gF", np.concatenate([_chunked(f("ffn1_norm")[0]), _chunked(f("ffn2_norm")[0]),
                              _chunked(f("ffn1_norm")[1]), _chunked(f("ffn2_norm")[1])], axis=1))
    put("gA", _chunked(f("attn_norm")[0]))
    put("gC", _chunked(f("conv_norm")[0]))
    put("qg", np.tile(f("attn_q_gain")[0], 2)[:, None])
    put("kg", np.tile(f("attn_k_gain")[0], 2)[:, None])
    table = f("attn_rel_bias")[0]
    put("bfar", np.broadcast_to(table[256][None, :], (128, NH)))
    put("bpw1", _chunked(f("conv_b_pw1")[0]))
    wdw = f("conv_w_dw")[0]
    put("wdw", np.ascontiguousarray(wdw.reshape(31, 8, 128).transpose(2, 1, 0)).reshape(128, 248))
    put("bdw", _chunked(f("conv_b_dw")[0]))
    put("lng", _chunked(f("conv_ln_g")[0]))
    put("lnb", _chunked(f("conv_ln_b")[0]))
    put("bpw2", _chunked(f("conv_b_pw2")[0]))
    biasd, maskd = _bias_tiles(table)
    cst = np.zeros((128, 384), np.float32)
    cst[:, 0:128] = np.eye(128, dtype=np.float32)
    cst[:, 128:256] = 1.0 / 1024.0
    cst[0:64, 256:320] = 1.0 / 64.0
    cst[64:128, 320:384] = 1.0 / 64.0

    shared = {
        "biasd": biasd, "maskd": maskd, "cst": cst,
        "wg1": f("ffn1_w_gate"), "wg2": f("ffn2_w_gate"),
        "wu1": f("ffn1_w_up"), "wu2": f("ffn2_w_up"),
        "wd1": f("ffn1_w_down"), "wd2": f("ffn2_w_down"),
        "wqkv": f("attn_w_qkv")[0], "wo": f("attn_w_o")[0],
        "wpw1": f("conv_w_pw1")[0], "wpw2": f("conv_w_pw2")[0],
    }
    in_maps = []
    for c in range(NCORES):
        b, qi = c // cps, c % cps
        s0 = qi * per
        xin = np.zeros((640 + per, D), np.float32)
        xin[0:64] = x_sample[c]
        if qi > 0:
            xin[64:640] = x_prompt[b, s0 - 576:s0]
        xin[640:] = x_prompt[b, s0:s0 + per]
        pr = par.copy()
        pr[:, PCOL["hval"]] = 1.0 if qi > 0 else 0.0
        pr[:, PCOL["eps"]] = EPS
        m = dict(shared)
        m.update({"xin": xin, "par": pr,
                  "ck": np.ascontiguousarray(ck_all[0, c].reshape(512, D)),
                  "cv": np.ascontiguousarray(cv_all[0, c].reshape(512, D)),
                  "sconv": np.ascontiguousarray(sc_all[0, c])})
        in_maps.append(m)

    nc = build_program(NT)
    res = run_bass_kernel_spmd(nc, in_maps, core_ids=list(range(NCORES)))
    R = res.results
    y_prompt = np.empty((B, S, D), np.float32)
    y_sample = np.empty((NCORES, 64, D), np.float32)
    nk_s = np.empty((1, NCORES, 64, NH, 64), np.float32)
    nv_s = np.empty((1, NCORES, 64, NH, 64), np.float32)
    ncv_s = np.empty((1, NCORES, 30, D), np.float32)
    nk_p = np.empty((1, B, 512, NH, 64), np.float32)
    nv_p = np.empty((1, B, 512, NH, 64), np.float32)
    ncv_p = np.empty((1, B, 30, D), np.float32)
    for c in range(NCORES):
        b, qi = c // cps, c % cps
        y_prompt[b, qi * per:(qi + 1) * per] = R[c]["y_own"]
        y_sample[c] = R[c]["y_s"]
        nk_s[0, c] = R[c]["o_ks"].reshape(64, NH, 64)
        nv_s[0, c] = R[c]["o_vs"].reshape(64, NH, 64)
        ncv_s[0, c] = R[c]["conv_s"]
        if qi == cps - 1:
            nk_p[0, b] = R[c]["o_klast"].reshape(512, NH, 64)
            nv_p[0, b] = R[c]["o_vlast"].reshape(512, NH, 64)
            ncv_p[0, b] = R[c]["conv_last"]
    return (y_prompt, y_sample, nk_p, nv_p, nk_s, nv_s, ncv_p, ncv_s)


def kernel(**inputs):
    return _run(inputs, NT_FULL)
```

```python
import contextlib
import numpy as np
import concourse.bass as bass
import concourse.mybir as mybir
from concourse.bass_utils import run_bass_kernel_spmd

F32 = mybir.dt.float32
BF16 = mybir.dt.bfloat16
AF = mybir.ActivationFunctionType
ALU = mybir.AluOpType

D = 1024
DFF = 2816
NH = 16
EPS = 1e-6
NEG = -30000.0
NCORES = 8
NT_FULL = 8


def _dsz(dt):
    return 2 if dt == BF16 else 4


class Ins:
    __slots__ = ("eng", "idx", "fn", "deps", "need_inc", "count", "dsem", "dval", "is_dma")

    def __init__(self, eng, idx, fn):
        self.eng = eng
        self.idx = idx
        self.fn = fn
        self.deps = []
        self.need_inc = False
        self.count = 0
        self.dsem = None
        self.dval = 0
        self.is_dma = False


class Prog:
    ENGS = ("pe", "act", "dve", "pool", "sp")

    def __init__(self, nc):
        self.nc = nc
        self.q = {e: [] for e in self.ENGS}
        self.recs = {}
        self.dma_cnt = {}
        self.pool_rr = {}
        self.pool_last = {}
        self.finals = []

    @staticmethod
    def region(ap):
        sp = str(ap.space)
        if "SB" not in sp and "PSUM" not in sp:
            return None
        a = ap.ap
        pstep, pcnt = a[0]
        off = ap.offset
        if pstep > 0:
            p0 = off // pstep
            f0 = off - p0 * pstep
        else:
            p0, f0 = 0, off
        ext = 0
        for st, c in a[1:]:
            ext += abs(st) * (c - 1)
        sz = _dsz(ap.dtype)
        if "PSUM" in sp:
            return (ap.name, 0, 128, 0, 2048)
        return (ap.name, p0, p0 + pcnt, f0 * sz, (f0 + ext + 1) * sz)

    def _access(self, ins, ap, write):
        r = self.region(ap)
        if r is None:
            return
        name, p0, p1, f0, f1 = r
        lst = self.recs.setdefault(name, [])
        keep = []
        for rec in lst:
            rp0, rp1, rf0, rf1, rins, rw = rec
            ov = (rp0 < p1 and p0 < rp1 and rf0 < f1 and f0 < rf1)
            if ov and (write or rw) and rins is not ins:
                ins.deps.append(rins)
            if write and ov and p0 <= rp0 and rp1 <= p1 and f0 <= rf0 and rf1 <= f1:
                continue
            if (not write) and (not rw) and rins.eng == ins.eng and (not rins.is_dma) and (not ins.is_dma) \
                    and rp0 == p0 and rp1 == p1 and rf0 == f0 and rf1 == f1:
                continue
            keep.append(rec)
        keep.append((p0, p1, f0, f1, ins, write))
        self.recs[name] = keep

    def op(self, eng, fn, reads=(), writes=(), dsem=None):
        ins = Ins(eng, len(self.q[eng]), fn)
        if dsem is not None:
            ins.is_dma = True
            ins.dsem = dsem
            self.dma_cnt[dsem] = self.dma_cnt.get(dsem, 0) + 16
            ins.dval = self.dma_cnt[dsem]
            prev = self.pool_last.get(dsem)
            if prev is not None:
                ins.deps.append(prev)
            self.pool_last[dsem] = ins
        for ap in reads:
            self._access(ins, ap, False)
        for ap in writes:
            self._access(ins, ap, True)
        self.q[eng].append(ins)
        return ins


PCOL = {}
_c = 0
for _nm, _w in (("gF", 32), ("gA", 8), ("gC", 8), ("qg", 1), ("kg", 1), ("bfar", 16), ("bpw1", 16),
                ("wdw", 248), ("bdw", 8), ("lng", 8), ("lnb", 8), ("bpw2", 8), ("hval", 1), ("eps", 1)):
    PCOL[_nm] = _c
    _c += _w
NPAR = _c


import os as _os


_OPT = _os.environ.get("KOPT", "aw")


def build_program(NT):
    nc = bass.Bass("TRN2", target_bir_lowering=False)
    P = Prog(nc)
    NROWS = 640 + NT * 512

    def din(name, shape):
        return nc.dram_tensor(name, list(shape), F32, kind="ExternalInput").ap()

    def dout(name, shape):
        return nc.dram_tensor(name, list(shape), F32, kind="ExternalOutput").ap()

    xin = din("xin", [NROWS, D])
    ck = din("ck", [512, D])
    cv = din("cv", [512, D])
    sconv = din("sconv", [30, D])
    par_d = din("par", [128, NPAR])
    bias_d = din("biasd", [128, NH * 2 * 128])
    mask_d = din("maskd", [128, 128])
    cst_d = din("cst", [128, 384])
    wg_d = [din("wg1", [2, D, DFF]), din("wg2", [2, D, DFF])]
    wu_d = [din("wu1", [2, D, DFF]), din("wu2", [2, D, DFF])]
    wd_d = [din("wd1", [2, DFF, D]), din("wd2", [2, DFF, D])]
    wqkv_d = din("wqkv", [D, 3 * D])
    wo_d = din("wo", [D, D])
    wpw1_d = din("wpw1", [D, 2 * D])
    wpw2_d = din("wpw2", [D, D])

    y_own = dout("y_own", [NT * 512, D])
    y_s = dout("y_s", [64, D])
    k_last = dout("o_klast", [512, D])
    v_last = dout("o_vlast", [512, D])
    k_s = dout("o_ks", [64, D])
    v_s = dout("o_vs", [64, D])
    conv_last = dout("conv_last", [30, D])
    conv_s = dout("conv_s", [30, D])

    es = contextlib.ExitStack()

    def sb(name, n, dt):
        return es.enter_context(nc.sbuf_tensor(name, [128, n], dt))

    with es:
        xT_A = sb("xT_A", 8 * 512, F32)
        xT_B = sb("xT_B", 8 * 128, F32)
        hT_A = sb("hT_A", 8 * 512, BF16)
        hT_B = sb("hT_B", 8 * 128, BF16)
        R1 = sb("R1", 22 * 512 // 2, F32)
        AT_B = sb("AT_B", 22 * 128, BF16)
        qT_B = sb("qT_B", 8 * 128, BF16)
        kwin = sb("kwin", 8 * 1152, BF16)
        vwin = sb("vwin", 9 * 1040, BF16)
        RS = sb("RS", 5184, F32)
        wsl = [sb("wsl%d" % i, 3072, BF16) for i in range(4)]
        sq = sb("sq", 8 * 512, BF16)
        rstd = sb("rstd", 512, F32)
        sg = [sb("sg%d" % i, 512, F32) for i in range(2)]
        xst = sb("xst", 1024, F32)
        ost = [sb("ost%d" % i, 1024, F32) for i in range(2)]
        biasT = sb("biasT", NH * 2 * 128, F32)
        maskT = sb("maskT", 128, F32)
        par = sb("par_sb", NPAR, F32)
        cst = sb("cst_sb", 384, F32)
        identb = sb("identb", 128, BF16)
        onesb = sb("onesb", 128, BF16)
        blkb = sb("blkb", 128, BF16)
        tmpq = sb("tmpq", 512, BF16)
        lnm = sb("lnm", 512, F32)
        lnr = sb("lnr", 512, F32)
        lnt = [sb("lnt%d" % i, 512, F32) for i in range(2)]
        tmpq2 = sb("tmpq2", 512, BF16)
        wrm = sb("wrm", 512, BF16)
        kn32s, rs2s, tmpqs = [lnt[0], lnm], [lnt[1], lnr], [tmpq, tmpq2]
        rden = sb("rden", 16, F32)
        chist = sb("chist", 8 * 30, F32)
        ebf_t = sb("ebf", 8 * 542, BF16)
        dgs = [AT_B[:, 2048 + i * 128: 2048 + (i + 1) * 128] for i in range(6)]
        ps = [es.enter_context(nc.psum_tensor("ps%d" % i, [128, 512], F32)) for i in range(8)]

        sems = {e: es.enter_context(nc.semaphore("s_" + e)) for e in ("pe", "act", "dve", "pool")}
        wsem = [es.enter_context(nc.semaphore("w%d" % i)) for i in range(4)]
        gsem = [es.enter_context(nc.semaphore("g%d" % i)) for i in range(8)]
        gs_i = [0]

        def next_gsem():
            s = gsem[gs_i[0] % len(gsem)]
            gs_i[0] += 1
            return s

        def v3(t, a, b):
            return t[:].rearrange("p (a b) -> p a b", a=a, b=b)

        xA = v3(xT_A, 8, 512)
        xB = v3(xT_B, 8, 128)
        hA = v3(hT_A, 8, 512)
        hB = v3(hT_B, 8, 128)
        R1b = R1[:].bitcast(BF16)
        ATA = R1b.rearrange("p (a b) -> p a b", a=22, b=512)
        ATB = v3(AT_B, 22, 128)
        qA = R1b[:, 0:4096].rearrange("p (a b) -> p a b", a=8, b=512)
        osb = R1b[:, 4096:5120]
        PT = [R1b[:, 5120 + i * 640: 5120 + (i + 1) * 640].rearrange("p (a b) -> p a b", a=5, b=128)
              for i in range(3)]
        snf = [R1[:, 3520 + i * 384: 3520 + (i + 1) * 384].rearrange("p (a b) -> p a b", a=3, b=128)
               for i in range(2)]
        ext = R1[:, 0:8 * 542].rearrange("p (a b) -> p a b", a=8, b=542)
        qB = v3(qT_B, 8, 128)
        kw = v3(kwin, 8, 1152)
        vw = vwin[:].rearrange("p (t h e) -> p t h e", t=9, h=16, e=65)
        RSb = RS[:].bitcast(BF16)
        ks = RSb[:, 0:5120].rearrange("p (a b) -> p a b", a=8, b=640)
        vs = RS[:, 2560:5160].bitcast(BF16).rearrange("p (t h e) -> p t h e", t=5, h=16, e=65)
        yc = RS[:, 0:4096].rearrange("p (a b) -> p a b", a=8, b=512)
        exs = RS[:, 2048:2048 + 752].rearrange("p (a b) -> p a b", a=8, b=94)
        uh = RS[:, 0:1024].rearrange("p (a b) -> p a b", a=8, b=128)
        ycs = RS[:, 1024:1536].rearrange("p (a b) -> p a b", a=8, b=64)
        chv = v3(chist, 8, 30)
        ebf = v3(ebf_t, 8, 542)
        ebfs = RS[:, 1536:1536 + 376].bitcast(BF16).rearrange("p (a b) -> p a b", a=8, b=94)
        bT = biasT[:].rearrange("p (h t q) -> p h t q", h=NH, t=2, q=128)
        identf = cst[:, 0:128]
        pc = lambda nm, i=0, n=1: par[:, PCOL[nm] + i: PCOL[nm] + i + n]
        wdw = par[:, PCOL["wdw"]:PCOL["wdw"] + 248].rearrange("p (k t) -> p k t", k=8, t=31)

        def mm(out, lhsT, rhs, start, stop):
            P.op("pe", lambda e: e.matmul(out, lhsT=lhsT, rhs=rhs, start=start, stop=stop),
                 reads=[lhsT, rhs], writes=[out])

        def tr(out, in_, ident):
            P.op("pe", lambda e: e.transpose(out, in_, ident), reads=[in_, ident], writes=[out])

        def act(out, in_, func, bias=None, scale=None):
            rd = [in_]
            kw_ = {}
            if bias is not None:
                kw_["bias"] = bias
                if not isinstance(bias, float):
                    rd.append(bias)
            if scale is not None:
                kw_["scale"] = scale
                if not isinstance(scale, float):
                    rd.append(scale)
            P.op("act", lambda e: e.activation(out=out, in_=in_, func=func, **kw_), reads=rd, writes=[out])

        def stt(eng, out, in0, scalar, in1, op0, op1):
            rd = [in0, in1] + ([] if isinstance(scalar, float) else [scalar])
            P.op(eng, lambda e: e.scalar_tensor_tensor(out=out, in0=in0, scalar=scalar, in1=in1, op0=op0, op1=op1),
                 reads=rd, writes=[out])

        def tsc(eng, out, in0, s1, s2, op0, op1=None):
            rd = [in0] + [s for s in (s1, s2) if s is not None and not isinstance(s, float)]
            if op1 is None:
                P.op(eng, lambda e: e.tensor_scalar(out=out, in0=in0, scalar1=s1, scalar2=None, op0=op0),
                     reads=rd, writes=[out])
            else:
                P.op(eng, lambda e: e.tensor_scalar(out=out, in0=in0, scalar1=s1, scalar2=s2, op0=op0, op1=op1),
                     reads=rd, writes=[out])

        def tt(eng, out, in0, in1, op):
            P.op(eng, lambda e: e.tensor_tensor(out=out, in0=in0, in1=in1, op=op), reads=[in0, in1], writes=[out])

        def cp(eng, out, in_):
            if eng == "act":
                P.op("act", lambda e: e.copy(out=out, in_=in_), reads=[in_], writes=[out])
            else:
                P.op(eng, lambda e: e.tensor_copy(out=out, in_=in_), reads=[in_], writes=[out])

        def recip(out, in_):
            P.op("dve", lambda e: e.reciprocal(out=out, in_=in_), reads=[in_], writes=[out])

        def rsqrt_eps(out, src):
            act(out, src, AF.Ln, bias=pc("eps"))
            act(out, out, AF.Exp, scale=-0.5)

        def mset(eng, ap, val):
            P.op(eng, lambda e: e.memset(ap, val), writes=[ap])

        def dma(q, out, in_, sem=None):
            s = sem if sem is not None else next_gsem()
            return P.op(q, lambda e: e.dma_start(out=out, in_=in_), reads=[in_], writes=[out], dsem=s)

        ws_i = [0]

        def wload(src3, kch, ncols):
            i = ws_i[0] % 4
            ws_i[0] += 1
            dst = wsl[i][:, 0:kch * ncols].rearrange("p (k f) -> p k f", k=kch, f=ncols)
            dma("pool", dst, src3, sem=wsem[i])
            return dst

        dma("sp", par[:], par_d[:, :])
        dma("sp", cst[:], cst_d[:, :])
        dma("sp", biasT[:], bias_d[:, :])
        dma("sp", maskT[:], mask_d[:, :])
        cp("dve", identb[:], cst[:, 0:128])
        cp("dve", onesb[:], cst[:, 128:256])
        cp("dve", blkb[:], cst[:, 256:384])
        for h in range(NH):
            tsc("dve", bT[:, h], bT[:, h], pc("bfar", h), None, ALU.subtract)
        mset("pool", kwin[:], 0.0)
        mset("dve", wrm[:], 0.0)
        mset("pool", vwin[:], 0.0)

        class Seg:
            def __init__(self, x, h, at, w, lo=0):
                self.x, self.h, self.at, self.w, self.lo = x, h, at, w, lo

            def sub(self, lo, w):
                return Seg(self.x, self.h, self.at, w, self.lo + lo)

            def X(self, k):
                return self.x[:, k, self.lo:self.lo + self.w]

            def H(self, k):
                return self.h[:, k, self.lo:self.lo + self.w]

            def A(self, j):
                return self.at[:, j, self.lo:self.lo + self.w]

        psi = {"g": 0, "u": 0, "d": 0}

        def prefetch_x(row0):
            for r in range(4):
                dma("sp", RS[:, r * 1024:(r + 1) * 1024], xin[row0 + r * 128: row0 + (r + 1) * 128, :])

        def load_x(seg, row0, pre=False):
            for r in range(seg.w // 128):
                if pre:
                    src = RS[:, r * 1024:(r + 1) * 1024]
                else:
                    src = xst[:]
                    dma("sp", xst[:], xin[row0 + r * 128: row0 + (r + 1) * 128, :])
                for half in range(2):
                    bank = ps[6 + half]
                    for kk in range(4):
                        k = half * 4 + kk
                        tr(bank[:, kk * 128:(kk + 1) * 128], src[:, k * 128:(k + 1) * 128], identf)
                    cp("act" if half == 0 else "dve",
                       seg.x[:, half * 4:half * 4 + 4, seg.lo + r * 128: seg.lo + (r + 1) * 128],
                       bank[:].rearrange("p (k t) -> p k t", k=4, t=128))

        def warm(n, bank):
            for _ in range(n):
                mm(bank[:, :], onesb[:], wrm[:], True, True)

        pre_stat = {}

        def stat_partial(s, k):
            w = s.w
            sqk = sq[:].rearrange("p (k t) -> p k t", k=8, t=512)[:, k, 0:w]
            act(sqk, s.X(k), AF.Square)
            if k >= 1:
                sqp = sq[:].rearrange("p (k t) -> p k t", k=8, t=512)[:, k - 1, 0:w]
                mm(ps[6][:, 0:w], onesb[:], sqp, k == 1, False)
            if k == 7:
                pre_stat[(id(s.x), s.lo, s.w)] = True

        def rmsnorm(segs, gcol):
            for s in segs:
                w = s.w
                if pre_stat.pop((id(s.x), s.lo, s.w), False) and len(segs) == 1:
                    sq7 = sq[:].rearrange("p (k t) -> p k t", k=8, t=512)[:, 7, 0:w]
                    mm(ps[6][:, 0:w], onesb[:], sq7, False, True)
                    if "w" in _OPT:
                        warm(20, ps[7])
                else:
                    sqv = sq[:].rearrange("p (k t) -> p k t", k=8, t=512)[:, :, 0:w]
                    act(sqv, s.x[:, :, s.lo:s.lo + w], AF.Square)
                    for k in range(8):
                        mm(ps[6][:, 0:w], onesb[:], sqv[:, k, :], k == 0, k == 7)
                rsqrt_eps(rstd[:, 0:w], ps[6][:, 0:w])
                for k in range(8):
                    stt("dve", s.H(k), s.X(k), par[:, gcol + k: gcol + k + 1],
                        rstd[:, 0:w], ALU.mult, ALU.mult)

        def wsrc(w2d, kch, c0, ncols):
            return w2d[:, c0:c0 + ncols].rearrange("(k p) f -> p k f", p=128)

        def ffn(segs, fi):
            li, which = fi // 2, fi % 2
            rmsnorm(segs, PCOL["gF"] + fi * 8)
            Wg, Wu, Wd = wg_d[which][li], wu_d[which][li], wd_d[which][li]
            for half in range(2):
                j0 = half * 11
                for g0, gn in ((0, 3), (3, 3), (6, 3), (9, 2)):
                    wgt = wload(wsrc(Wg, 8, (j0 + g0) * 128, gn * 128), 8, gn * 128)
                    wut = wload(wsrc(Wu, 8, (j0 + g0) * 128, gn * 128), 8, gn * 128)
                    for n in range(gn):
                        j = g0 + n
                        for s in segs:
                            w = s.w
                            gb = ps[psi["g"] % 2]
                            ub = ps[2 + psi["g"] % 2]
                            sgt = sg[psi["g"] % 2]
                            psi["g"] += 1
                            for k in range(8):
                                mm(gb[:, 0:w], wgt[:, k, n * 128:(n + 1) * 128], s.H(k), k == 0, k == 7)
                            for k in range(8):
                                mm(ub[:, 0:w], wut[:, k, n * 128:(n + 1) * 128], s.H(k), k == 0, k == 7)
                            act(sgt[:, 0:w], gb[:, 0:w], AF.Silu)
                            tt("dve", s.A(j), sgt[:, 0:w], ub[:, 0:w], ALU.mult)
                for n0 in range(0, 8, 2):
                    wdt = wload(Wd[j0 * 128:(j0 + 11) * 128, n0 * 128:(n0 + 2) * 128]
                                .rearrange("(k p) f -> p k f", p=128), 11, 256)
                    for n in range(2):
                        for s in segs:
                            w = s.w
                            db = ps[4 + psi["d"] % 2]
                            psi["d"] += 1
                            for j in range(11):
                                mm(db[:, 0:w], wdt[:, j, n * 128:(n + 1) * 128], s.A(j), j == 0, j == 10)
                            stt("dve", s.X(n0 + n), db[:, 0:w], 0.5, s.X(n0 + n), ALU.mult, ALU.add)
                            if half == 1 and fi != 3 and len(segs) == 1 and s.w == 512:
                                stat_partial(s, n0 + n)

        def linear8(segs, W2d, c0, nout, evac):
            for g0 in range(0, nout, 3):
                gn = min(3, nout - g0)
                wt = wload(wsrc(W2d, 8, c0 + g0 * 128, gn * 128), 8, gn * 128)
                for n in range(gn):
                    for s in segs:
                        b = ps[psi["g"] % 4]
                        psi["g"] += 1
                        for k in range(8):
                            mm(b[:, 0:s.w], wt[:, k, n * 128:(n + 1) * 128], s.H(k), k == 0, k == 7)
                        evac(g0 + n, s, b[:, 0:s.w])

        def out_rows(dst_rows, src_fn, ncol, eng_i=[0]):
            o = ost[eng_i[0] % 2]
            eng_i[0] += 1
            for half in range(2):
                bank = ps[6 + half]
                for kk in range(4):
                    tr(bank[0:ncol, kk * 128:(kk + 1) * 128], src_fn(half * 4 + kk), identf)
                cp("act" if half == 0 else "dve", o[0:ncol, half * 512:(half + 1) * 512], bank[0:ncol, :])
            d = dma("sp", dst_rows, o[0:ncol, :])
            P.finals.append(d)

        def qkv(segs, kdst, vdst, kout, vout, onescol):
            rmsnorm(segs, PCOL["gA"])

            pend = []
            qcnt = [0]

            def evac_qk(n, s, pb):
                i = qcnt[0]
                qcnt[0] += 1
                act(tmpqs[i % 2][:, 0:s.w], pb, AF.Square)
                if pend:
                    evac_qk2(*pend.pop())
                pend.append((i, n, s, pb))

            def evac_qk2(i, n, s, pb):
                w = s.w
                isk = n >= 8
                hp = n % 8
                tmpq_, rs2, kn32 = tmpqs[i % 2], rs2s[i % 2], kn32s[i % 2]
                mm(ps[4 + i % 2][:, 0:w], blkb[:], tmpq_[:, 0:w], True, True)
                rsqrt_eps(rs2[:, 0:w], ps[4 + i % 2][:, 0:w])
                gcol = pc("kg") if isk else pc("qg")
                if not isk:
                    stt("dve", s.qdst[:, hp, s.lo:s.lo + w], pb, gcol, rs2[:, 0:w], ALU.mult, ALU.mult)
                else:
                    stt("dve", kn32[:, 0:w], pb, gcol, rs2[:, 0:w], ALU.mult, ALU.mult)
                    for (dst, lo, hi) in kdst(s, hp):
                        cp("dve" if "c" in _OPT else "act", dst, kn32[:, lo:hi])
                    ko = kout(s) if "k" not in _os.environ.get("KSKIP", "") else None
                    if ko is not None:
                        rows, lo, ncol, kst = ko
                        for ci, c in enumerate(range(0, ncol, 128)):
                            cw = min(128, ncol - c)
                            tr(ps[7][0:cw, 0:128], kn32[:, lo + c: lo + c + cw], identf)
                            cp("act", kst[ci][0:cw, hp * 128:(hp + 1) * 128], ps[7][0:cw, 0:128])
                            if hp == 7:
                                d = dma("sp", rows[c:c + cw, :], kst[ci][0:cw, :])
                                P.finals.append(d)

            linear8(segs, wqkv_d, 0, 16, evac_qk)
            evac_qk2(*pend.pop())
            for c0, ncg in ((0, 384), (384, 384), (768, 256)):
                wt = wload(wsrc(wqkv_d, 8, 2048 + c0, ncg), 8, ncg)
                h0, nhg = c0 // 64, ncg // 64
                for s in segs:
                    for r in range(s.w // 128):
                        b = ps[psi["g"] % 4]
                        psi["g"] += 1
                        for k in range(8):
                            mm(b[:, 0:ncg], s.h[:, k, s.lo + r * 128: s.lo + (r + 1) * 128], wt[:, k, :], k == 0, k == 7)
                        bv = b[:, 0:ncg].rearrange("p (h e) -> p h e", h=nhg, e=64)
                        for (dst, lo, hi) in vdst(s, r):
                            cp("act" if (r % 2 == 0 or "c" not in _OPT) else "dve", dst[lo:hi, h0:h0 + nhg, 0:64], bv[lo:hi])
                        vo = vout(s, r) if "v" not in _os.environ.get("KSKIP", "") else None
                        if vo is not None:
                            rows, lo, nrow, vst = vo
                            cp("act", vst[lo:lo + nrow, c0:c0 + ncg], b[lo:lo + nrow, 0:ncg])
                            if c0 == 768 and "d" not in _os.environ.get("KSKIP", ""):
                                d = dma("sp", rows[:, :], vst[lo:lo + nrow, :])
                                P.finals.append(d)
            onescol()

        pti = [0]

        def attn_pair(kT, vT, t0, qsrc, q0, nq, odst, o0):
            obanks = (ps[4], ps[5], ps[6])
            pts = {}

            def stage1(h):
                hp, po = h // 2, (h % 2) * 64
                nb = ps[h % 2]
                fb = ps[2 + h % 2]
                q = qsrc[po:po + 64, hp, q0:q0 + nq]
                for i, j in enumerate((0, 3, 4)):
                    mm(nb[:, i * 128:i * 128 + nq], kT[po:po + 64, hp, (t0 + j) * 128:(t0 + j + 1) * 128], q, True, True)
                for i, j in enumerate((1, 2)):
                    mm(fb[:, i * 128:i * 128 + nq], kT[po:po + 64, hp, (t0 + j) * 128:(t0 + j + 1) * 128], q, True, True)
                pt = PT[pti[0] % 3]
                sn = snf[pti[0] % 2]
                pti[0] += 1
                pts[h] = pt
                nbv = nb[:, 0:384].rearrange("p (a b) -> p a b", a=3, b=128)
                fbv = fb[:, 0:256].rearrange("p (a b) -> p a b", a=2, b=128)
                stt("dve", sn[:, 0, 0:nq], nbv[:, 0, 0:nq], 0.125, maskT[:, 0:nq], ALU.mult, ALU.add)
                stt("dve", sn[:, 1:3, 0:nq], nbv[:, 1:3, 0:nq], 0.125, bT[:, h, :, 0:nq], ALU.mult, ALU.add)
                act(pt[:, 0:3, 0:nq], sn[:, :, 0:nq], AF.Exp)
                act(pt[:, 3:5, 0:nq], fbv[:, :, 0:nq], AF.Exp, scale=0.125)

            def stage2(h):
                pt = pts[h]
                ob = obanks[h // 7]
                oc = (h % 7) * 65
                for i, j in enumerate((0, 3, 4, 1, 2)):
                    mm(ob[0:nq, oc:oc + 65], pt[:, i, 0:nq], vT[:, t0 + j, h, :], i == 0, i == 4)

            stage1(0)
            for h in range(NH):
                if h + 1 < NH:
                    stage1(h + 1)
                stage2(h)
            for g, (h0, nh) in enumerate(((0, 7), (7, 7), (14, 2))):
                ov = obanks[g][0:nq, 0:nh * 65].rearrange("p (h e) -> p h e", h=nh, e=65)
                P.op("dve", lambda e, ov=ov, h0=h0, nh=nh: e.reciprocal(out=rden[0:nq, h0:h0 + nh], in_=ov[:, :, 64]),
                     reads=[ov[:, :, 64]], writes=[rden[0:nq, h0:h0 + nh]])
                tt("dve", osb[0:nq, h0 * 64:(h0 + nh) * 64].rearrange("p (h e) -> p h e", h=nh, e=64),
                   ov[:, :, 0:64], rden[0:nq, h0:h0 + nh].unsqueeze(2).to_broadcast([nq, nh, 64]), ALU.mult)
            tb = ps[7][:].bitcast(BF16)
            for k in range(8):
                tr(tb[:, k * 128:k * 128 + nq], osb[0:nq, k * 128:(k + 1) * 128], identb[0:nq, 0:nq])
            cp("act", odst[:, :, o0:o0 + nq], tb.rearrange("p (k t) -> p k t", k=8, t=128)[:, :, 0:nq])

        def wo_proj(segs):
            def ev(n, s, pb):
                tt("dve", s.X(n), pb, s.X(n), ALU.add)
                if len(segs) == 1 and s.w == 512:
                    stat_partial(s, n)
            linear8(segs, wo_d, 0, 8, ev)

        def pw1_glu(segs, udst):
            rmsnorm(segs, PCOL["gC"])
            for g0, gn in ((0, 3), (3, 3), (6, 2)):
                wa = wload(wsrc(wpw1_d, 8, g0 * 128, gn * 128), 8, gn * 128)
                wgt = wload(wsrc(wpw1_d, 8, 1024 + g0 * 128, gn * 128), 8, gn * 128)
                for n in range(gn):
                    for s in segs:
                        w = s.w
                        ab = ps[psi["g"] % 2]
                        gb = ps[2 + psi["g"] % 2]
                        sgt = sg[psi["g"] % 2]
                        psi["g"] += 1
                        for k in range(8):
                            mm(ab[:, 0:w], wa[:, k, n * 128:(n + 1) * 128], s.H(k), k == 0, k == 7)
                        for k in range(8):
                            mm(gb[:, 0:w], wgt[:, k, n * 128:(n + 1) * 128], s.H(k), k == 0, k == 7)
                        act(sgt[:, 0:w], gb[:, 0:w], AF.Sigmoid, bias=pc("bpw1", 8 + g0 + n))
                        stt("dve", udst(s, g0 + n), ab[:, 0:w], pc("bpw1", g0 + n), sgt[:, 0:w], ALU.add, ALU.mult)

        dgi = [0]

        def dwconv_ln(e3, eb3, ycv, w, hdst):
            cp("act", eb3[:, :, :], e3[:, :, :])
            sqv = sq[:].rearrange("p (k t) -> p k t", k=8, t=512)[:, :, 0:w]

            def ln_stats(k):
                mm(ps[6][:, 0:w], onesb[:], tmpqs[k % 2][:, 0:w], k == 0, k == 7)
                mm(ps[7][:, 0:w], onesb[:], sqv[:, k, :], k == 0, k == 7)

            for k in range(8):
                cb = ps[k % 4]
                for t in range(31):
                    dg = dgs[dgi[0] % 6]
                    dgi[0] += 1
                    if t % 2 == 0 or "b" in _OPT:
                        tsc("dve", dg, identb[:], wdw[:, k, t:t + 1], None, ALU.mult)
                    else:
                        act(dg, identb[:], AF.Copy, scale=wdw[:, k, t:t + 1])
                    mm(cb[:, 0:w], dg, eb3[:, k, t:t + w], t == 0, t == 30)
                tsc("dve", ycv[:, k, 0:w], cb[:, 0:w], pc("bdw", k), None, ALU.add)
                cp("act", tmpqs[k % 2][:, 0:w], ycv[:, k, 0:w])
                act(sqv[:, k, :], ycv[:, k, 0:w], AF.Square)
                if k >= 1:
                    ln_stats(k - 1)
            ln_stats(7)
            if "w" in _OPT and w == 512:
                warm(40, ps[4])
            cp("dve", lnm[:, 0:w], ps[6][:, 0:w])
            tt("dve", lnr[:, 0:w], lnm[:, 0:w], lnm[:, 0:w], ALU.mult)
            tt("dve", lnr[:, 0:w], ps[7][:, 0:w], lnr[:, 0:w], ALU.subtract)
            tsc("dve", lnr[:, 0:w], lnr[:, 0:w], 0.0, EPS, ALU.max, ALU.add)
            act(lnr[:, 0:w], lnr[:, 0:w], AF.Ln)
            act(lnr[:, 0:w], lnr[:, 0:w], AF.Exp, scale=-0.5)
            for k in range(8):
                t_ = lnt[k % 2]
                tt("dve", t_[:, 0:w], ycv[:, k, 0:w], lnm[:, 0:w], ALU.subtract)
                tt("dve", t_[:, 0:w], t_[:, 0:w], lnr[:, 0:w], ALU.mult)
                act(hdst(k), t_[:, 0:w], AF.Silu, bias=pc("lnb", k), scale=pc("lng", k))

        def pw2_proj(segs):
            def ev(n, s, pb):
                stt("dve", s.X(n), pb, pc("bpw2", n), s.X(n), ALU.add, ALU.add)
                if len(segs) == 1 and s.w == 512:
                    stat_partial(s, n)
            linear8(segs, wpw2_d, 0, 8, ev)

        _STOP = int(_os.environ.get("KSTOP", "0"))

        class _Stop(Exception):
            pass

        def chk(n):
            if _STOP == n:
                raise _Stop()

        try:
            A = Seg(xA, hA, ATA, 512)
            A.qdst = qA
            B = Seg(xB, hB, ATB, 128)
            B.qdst = qB
            load_x(B, 0)
            load_x(A, 128)
            chk(1)
            ffn([A, B], 0)
            chk(2)

            hv = pc("hval")

            def kdst_H(s, hp):
                if s is A:
                    return [(kw[:, hp, 128:640], 0, 512)]
                return [(ks[:, hp, 512:576], 0, 64), (kw[:, hp, 64:128], 64, 128)]

            def vdst_H(s, r):
                if s is A:
                    return [(vw[:, 1 + r], 0, 128)]
                return [(vs[:, 4], 0, 64), (vw[:, 0], 64, 128)]

            def ones_H():
                mset("dve", vw[:, 1:5, :, 64], 1.0)
                mset("dve", vw[64:128, 0, :, 64], 1.0)

            mset("dve", RS[:], 0.0)
            for t in range(4 if "c" not in _os.environ.get("KSKIP", "") else 0):
                dma("sp", ost[t % 2][:], cv[t * 128:(t + 1) * 128, :])
                cp("dve", vs[:, t, :, 0:64], ost[t % 2][:].rearrange("p (h e) -> p h e", h=16, e=64))
                mset("dve", vs[:, t, :, 64], 1.0)
            mset("dve", vs[0:64, 4, :, 64], 1.0)
            for t in range(4 if "c" not in _os.environ.get("KSKIP", "") else 0):
                dma("sp", xst[:], ck[t * 128:(t + 1) * 128, :])
                for half in range(2):
                    bank = ps[6 + half]
                    for kk in range(4):
                        k = half * 4 + kk
                        tr(bank[:, kk * 128:(kk + 1) * 128], xst[:, k * 128:(k + 1) * 128], identf)
                    cp("act" if half == 0 else "dve", ks[:, half * 4:half * 4 + 4, t * 128:(t + 1) * 128],
                       bank[:].rearrange("p (k t) -> p k t", k=4, t=128))

            qkv([A, B], kdst_H, vdst_H,
                lambda s: (k_s, 0, 64, [ost[0][:]]) if s is B else None,
                lambda s, r: (v_s, 0, 64, xst[:]) if s is B else None,
                ones_H)
            chk(3)
            attn_pair(kw, vw, 0, qA, 384, 128, hA, 384)
            for t in range(1, 5):
                cp("dve", vw[:, t, :, 64], hv.to_broadcast([128, 16]))
            attn_pair(ks, vs, 0, qB, 0, 64, hB, 0)
            chk(4)
            cp("dve", xB[:, :, 64:128], xA[:, :, 448:512])
            cp("act", hB[:, :, 64:128], hA[:, :, 448:512])
            if "a" in _OPT:
                prefetch_x(640)
            B2 = B.sub(0, 64)
            chk(7)

            A.qdst = qA
            for ti in range(NT):
                last = ti == NT - 1
                load_x(A, 640 + ti * 512, pre=("a" in _OPT))
                ffn([A], 0)
                qkv([A],
                    lambda s, hp: [(kw[:, hp, 640:1152], 0, 512)],
                    lambda s, r: [(vw[:, 5 + r], 0, 128)],
                    (lambda s: (k_last, 0, 512, [xT_B[:], AT_B[:].bitcast(F32)[:, 0:1024],
                                                 sq[:].bitcast(F32)[:, 0:1024], sq[:].bitcast(F32)[:, 1024:2048]]))
                    if last else (lambda s: None),
                    (lambda s, r: (v_last[r * 128:(r + 1) * 128, :], 0, 128, RS[:, r * 1024:(r + 1) * 1024]))
                    if last else (lambda s, r: None),
                    lambda: mset("dve", vw[:, 5:9, :, 64], 1.0))
                for p in range(4):
                    attn_pair(kw, vw, 1 + p, qA, p * 128, 128, hA, p * 128)
                cp("act", kw[:, :, 128:640], kw[:, :, 640:1152])
                cp("dve", vw[:, 1:5], vw[:, 5:9])
                first = ti == 0
                post = [A, B] if first else [A]
                wo_proj(post)
                ffn(post, 1)
                ffn(post, 2)
                if first:
                    dma("sp", xst[0:30, :], sconv[:, :])
                    for half in range(2):
                        bank = ps[6 + half]
                        for kk in range(4):
                            k = half * 4 + kk
                            tr(bank[:, kk * 128:kk * 128 + 30], xst[0:30, k * 128:(k + 1) * 128], identf[0:30, 0:30])
                        cp("act" if half == 0 else "dve", exs[:, half * 4:half * 4 + 4, 0:30],
                           bank[:].rearrange("p (k t) -> p k t", k=4, t=128)[:, :, 0:30])
                pw1_glu(post, lambda s, n: ext[:, n, 30:542] if s is A else uh[:, n, :])
                if first:
                    tsc("dve", chv, uh[:, :, 98:128], hv, None, ALU.mult)
                    cp("dve", exs[:, :, 30:94], uh[:, :, 0:64])
                    out_rows(conv_s[:, :], lambda k: exs[:, k, 64:94], 30)
                    dwconv_ln(exs, ebfs, ycs, 64, lambda k: hB[:, k, 0:64])
                cp("dve", ext[:, :, 0:30], chv)
                if last:
                    out_rows(conv_last[:, :], lambda k: ext[:, k, 512:542], 30)
                dwconv_ln(ext, ebf, yc, 512, lambda k: hA[:, k, :])
                if not last and "a" in _OPT:
                    prefetch_x(640 + (ti + 1) * 512)
                cp("dve", chv, ext[:, :, 512:542])
                post2 = [A, B2] if first else [A]
                pw2_proj(post2)
                ffn(post2, 3)
                if first:
                    out_rows(y_s[:, :], lambda k: xB[:, k, 0:64], 64)
                for r in range(4):
                    out_rows(y_own[ti * 512 + r * 128: ti * 512 + (r + 1) * 128, :],
                             lambda k, r=r: xA[:, k, r * 128:(r + 1) * 128], 128)

        except _Stop:
            pass
        with nc.Block() as block:
            @block.tensor
            def _(e):
                _emit_engine(P, "pe", e, sems)

            @block.scalar
            def _(e):
                _emit_engine(P, "act", e, sems)

            @block.vector
            def _(e):
                _emit_engine(P, "dve", e, sems)

            @block.gpsimd
            def _(e):
                _emit_engine(P, "pool", e, sems)

            @block.sync
            def _(e):
                _emit_engine(P, "sp", e, sems)
    return nc


_PREP = {}


def _emit_engine(P, eng, h, sems):
    if id(P) not in _PREP:
        _PREP.clear()
        _PREP[id(P)] = True
        for e in P.ENGS:
            for c in P.q[e]:
                real = []
                for p in c.deps:
                    if p.is_dma:
                        real.append(p)
                    elif p.eng == c.eng:
                        if c.is_dma or c.eng != "pe":
                            real.append(p)
                            p.need_inc = True
                    else:
                        real.append(p)
                        p.need_inc = True
                c.deps = real
        for e in P.ENGS:
            n = 0
            for c in P.q[e]:
                if c.need_inc and not c.is_dma:
                    n += 1
                    c.count = n
    waited = {}
    for c in P.q[eng]:
        need = {}
        for p in c.deps:
            if p.is_dma:
                k, v = p.dsem, p.dval
            else:
                k, v = sems[p.eng], p.count
            if v > need.get(k, 0):
                need[k] = v
        for k, v in need.items():
            if v > waited.get(k, 0):
                h.wait_ge(k, v)
                waited[k] = v
        bi = c.fn(h)
        if c.is_dma:
            bi.then_inc(c.dsem, 16)
        elif c.need_inc:
            bi.then_inc(sems[eng], 1)
    if eng == "sp":
        for p in P.finals:
            if p.dval > waited.get(p.dsem, 0):
                h.wait_ge(p.dsem, p.dval)
                waited[p.dsem] = p.dval


def _chunked(v):
    v = np.asarray(v, np.float32).reshape(-1, 128)
    return np.ascontiguousarray(v.T)


def _bias_tiles(table):
    kk = np.arange(128)[:, None]
    qq = np.arange(128)[None, :]
    out = np.empty((128, NH, 2, 128), np.float32)
    for t, j in enumerate((3, 4)):
        dist = 512 + qq - (j * 128 + kk)
        qc = qq // 64
        m = (j * 128 + kk) // 64
        valid = (m >= qc) & (m <= qc + 8)
        idx = np.clip(dist, -128, 128) + 128
        vals = table[idx]
        vals = np.where(valid[:, :, None], vals, np.float32(NEG))
        out[:, :, t, :] = np.transpose(vals, (0, 2, 1))
    m0 = kk // 64
    qc = qq // 64
    maskd = np.where((m0 >= qc), np.float32(0.0), np.float32(NEG)).astype(np.float32)
    maskd = np.broadcast_to(maskd, (128, 128)).copy()
    return out.reshape(128, NH * 2 * 128), maskd


def _run(inputs, NT):
    f = lambda k: np.asarray(inputs[k], np.float32)
    x_prompt, x_sample = f("x_prompt"), f("x_sample")
    ck_all, cv_all, sc_all = f("cache_attn_k"), f("cache_attn_v"), f("state_conv")
    B, S, _ = x_prompt.shape
    per = NT * 512
    cps = S // per
    assert B * cps == NCORES and x_sample.shape[0] == NCORES

    par = np.zeros((128, NPAR), np.float32)

    def put(nm, arr):
        arr = np.asarray(arr, np.float32)
        par[:, PCOL[nm]:PCOL[nm] + arr.shape[1]] = arr

    put("gF", np.concatenate([_chunked(f("ffn1_norm")[0]), _chunked(f("ffn2_norm")[0]),
                              _chunked(f("ffn1_norm")[1]), _chunked(f("ffn2_norm")[1])], axis=1))
    put("gA", _chunked(f("attn_norm")[0]))
    put("gC", _chunked(f("conv_norm")[0]))
    put("qg", np.tile(f("attn_q_gain")[0], 2)[:, None])
    put("kg", np.tile(f("attn_k_gain")[0], 2)[:, None])
    table = f("attn_rel_bias")[0]
    put("bfar", np.broadcast_to(table[256][None, :], (128, NH)))
    put("bpw1", _chunked(f("conv_b_pw1")[0]))
    wdw = f("conv_w_dw")[0]
    put("wdw", np.ascontiguousarray(wdw.reshape(31, 8, 128).transpose(2, 1, 0)).reshape(128, 248))
    put("bdw", _chunked(f("conv_b_dw")[0]))
    put("lng", _chunked(f("conv_ln_g")[0]))
    put("lnb", _chunked(f("conv_ln_b")[0]))
    put("bpw2", _chunked(f("conv_b_pw2")[0]))
    biasd, maskd = _bias_tiles(table)
    cst = np.zeros((128, 384), np.float32)
    cst[:, 0:128] = np.eye(128, dtype=np.float32)
    cst[:, 128:256] = 1.0 / 1024.0
    cst[0:64, 256:320] = 1.0 / 64.0
    cst[64:128, 320:384] = 1.0 / 64.0

    shared = {
        "biasd": biasd, "maskd": maskd, "cst": cst,
        "wg1": f("ffn1_w_gate"), "wg2": f("ffn2_w_gate"),
        "wu1": f("ffn1_w_up"), "wu2": f("ffn2_w_up"),
        "wd1": f("ffn1_w_down"), "wd2": f("ffn2_w_down"),
        "wqkv": f("attn_w_qkv")[0], "wo": f("attn_w_o")[0],
        "wpw1": f("conv_w_pw1")[0], "wpw2": f("conv_w_pw2")[0],
    }
    in_maps = []
    for c in range(NCORES):
        b, qi = c // cps, c % cps
        s0 = qi * per
        xin = np.zeros((640 + per, D), np.float32)
        xin[0:64] = x_sample[c]
        if qi > 0:
            xin[64:640] = x_prompt[b, s0 - 576:s0]
        xin[640:] = x_prompt[b, s0:s0 + per]
        pr = par.copy()
        pr[:, PCOL["hval"]] = 1.0 if qi > 0 else 0.0
        pr[:, PCOL["eps"]] = EPS
        m = dict(shared)
        m.update({"xin": xin, "par": pr,
                  "ck": np.ascontiguousarray(ck_all[0, c].reshape(512, D)),
                  "cv": np.ascontiguousarray(cv_all[0, c].reshape(512, D)),
                  "sconv": np.ascontiguousarray(sc_all[0, c])})
        in_maps.append(m)

    nc = build_program(NT)
    res = run_bass_kernel_spmd(nc, in_maps, core_ids=list(range(NCORES)))
    R = res.results
    y_prompt = np.empty((B, S, D), np.float32)
    y_sample = np.empty((NCORES, 64, D), np.float32)
    nk_s = np.empty((1, NCORES, 64, NH, 64), np.float32)
    nv_s = np.empty((1, NCORES, 64, NH, 64), np.float32)
    ncv_s = np.empty((1, NCORES, 30, D), np.float32)
    nk_p = np.empty((1, B, 512, NH, 64), np.float32)
    nv_p = np.empty((1, B, 512, NH, 64), np.float32)
    ncv_p = np.empty((1, B, 30, D), np.float32)
    for c in range(NCORES):
        b, qi = c // cps, c % cps
        y_prompt[b, qi * per:(qi + 1) * per] = R[c]["y_own"]
        y_sample[c] = R[c]["y_s"]
        nk_s[0, c] = R[c]["o_ks"].reshape(64, NH, 64)
        nv_s[0, c] = R[c]["o_vs"].reshape(64, NH, 64)
        ncv_s[0, c] = R[c]["conv_s"]
        if qi == cps - 1:
            nk_p[0, b] = R[c]["o_klast"].reshape(512, NH, 64)
            nv_p[0, b] = R[c]["o_vlast"].reshape(512, NH, 64)
            ncv_p[0, b] = R[c]["conv_last"]
    return (y_prompt, y_sample, nk_p, nv_p, nk_s, nv_s, ncv_p, ncv_s)


def kernel(**inputs):
    return _run(inputs, NT_FULL)
```
